# Optimizing a Trainium2 kernel written in Bass

```python
import math
import jax, jax.numpy as jnp
from jax import lax
import numpy as np

D_MODEL = 1024
BATCH = 8
SEQ = 4096
DEPTH = 2

CHUNK = 64
Q_BLOCK = 128
POOL_WINDOWS = (2, 4, 8, 16)
N_POOL_GROUPS = len(POOL_WINDOWS)
POOL_GROUP_DIM = D_MODEL // 8
POOL_DIM = N_POOL_GROUPS * POOL_GROUP_DIM
N_HEADS = 8
HEAD_DIM = D_MODEL // 16
V_HEAD_DIM = 2 * HEAD_DIM
QK_DIM = N_HEADS * 2 * HEAD_DIM
V_DIM = N_HEADS * V_HEAD_DIM
IN_DIM = POOL_DIM + 2 * QK_DIM + V_DIM
D_FF = 4 * D_MODEL
N_BRANCHES = 2
EPS = 1e-6

kernel_name = "hybrid_pool_diffattn_gated_encoder"


def rmsnorm(x, g):
    xf = x.astype(jnp.float32)
    y = xf * lax.rsqrt(jnp.mean(xf * xf, axis=-1, keepdims=True) + EPS)
    return (y * g.astype(jnp.float32)).astype(x.dtype)


def head_rmsnorm(x, g):
    xf = x.astype(jnp.float32)
    return xf * lax.rsqrt(jnp.mean(xf * xf, axis=-1, keepdims=True) + EPS) * g.astype(jnp.float32)


def alibi_slopes(n):
    return jnp.asarray(np.array([2.0 ** (-8.0 * (i + 1) / n) for i in range(n)], dtype=np.float32))


def pool_mixer(u, w_grp, scale):
    B, S, _ = u.shape
    uf = u.astype(jnp.float32).reshape(B, S, N_POOL_GROUPS, POOL_GROUP_DIM)
    csum = jnp.pad(jnp.cumsum(uf, axis=1), ((0, 0), (1, 0), (0, 0), (0, 0)))
    t = jnp.arange(S)
    pooled = []
    for g, w in enumerate(POOL_WINDOWS):
        lo = jnp.maximum(t + 1 - w, 0)
        win_sum = csum[:, 1:, g] - csum[:, lo, g]
        cnt = (t + 1 - lo).astype(jnp.float32)
        pooled.append(win_sum / cnt[None, :, None])
    mixed = jnp.stack(pooled, axis=2) - uf
    y = jnp.einsum('bsgc,gcd->bsgd', mixed, w_grp.astype(jnp.float32))
    return (y.reshape(B, S, POOL_DIM) * scale.astype(jnp.float32)).astype(u.dtype)


def diff_attention(q, k, v, g_q, g_k, lam, g_sub, lambda_init):
    B, S = q.shape[0], q.shape[1]
    qn = head_rmsnorm(q, g_q) * (HEAD_DIM ** -0.5)
    kn = head_rmsnorm(k, g_k)
    vf = v.astype(jnp.float32)
    lam = lam.astype(jnp.float32)
    slopes = alibi_slopes(N_HEADS)
    pos = jnp.arange(S)
    key_chunk = pos // CHUNK
    nb = S // Q_BLOCK
    q_blocks = qn.reshape(B, nb, Q_BLOCK, N_HEADS, 2, HEAD_DIM).transpose(1, 0, 2, 3, 4, 5)
    q_pos = pos.reshape(nb, Q_BLOCK)
    neg = jnp.finfo(jnp.float32).min

    def block(args):
        qblk, qp = args
        s = jnp.einsum('bqhid,bkhid->bihqk', qblk, kn)
        dist = jnp.abs(qp[:, None] - pos[None, :]).astype(jnp.float32)
        bias = -slopes[:, None, None] * dist
        allowed = key_chunk[None, :] <= (qp // CHUNK)[:, None]
        s = jnp.where(allowed, s + bias, neg)
        p = jax.nn.softmax(s, axis=-1)
        a = p[:, 0] - lam * p[:, 1]
        return jnp.einsum('bhqk,bkhe->bqhe', a, vf)

    o = lax.map(block, (q_blocks, q_pos))
    o = o.transpose(1, 0, 2, 3, 4).reshape(B, S, N_HEADS, V_HEAD_DIM)
    o = head_rmsnorm(o, g_sub) * (1.0 - lambda_init)
    return o.reshape(B, S, V_DIM).astype(v.dtype)


def setup_inputs(seed: int = 0) -> dict:
    key = jax.random.key(seed)
    ks = jax.random.split(key, 20)
    f32 = jnp.float32

    def nrm(k, shape, fan_in):
        return jax.random.normal(k, shape, f32) * (fan_in ** -0.5)

    def gain(k, shape):
        return 1.0 + 0.05 * jax.random.normal(k, shape, f32)

    return {
        "x": jax.random.normal(ks[0], (BATCH, SEQ, D_MODEL), f32),
        "g_mix": gain(ks[1], (DEPTH, D_MODEL)),
        "w_in": nrm(ks[2], (DEPTH, D_MODEL, IN_DIM), D_MODEL),
        "w_pool_grp": nrm(ks[3], (DEPTH, N_POOL_GROUPS, POOL_GROUP_DIM, POOL_GROUP_DIM), POOL_GROUP_DIM),
        "pool_scale": gain(ks[4], (DEPTH, POOL_DIM)),
        "g_q": gain(ks[5], (DEPTH, HEAD_DIM)),
        "g_k": gain(ks[6], (DEPTH, HEAD_DIM)),
        "lambda_qk": 0.1 * jax.random.normal(ks[7], (DEPTH, 4, HEAD_DIM), f32),
        "g_sub": gain(ks[8], (DEPTH, V_HEAD_DIM)),
        "w_branch_pool": nrm(ks[9], (DEPTH, POOL_DIM, D_MODEL), POOL_DIM),
        "w_branch_attn": nrm(ks[10], (DEPTH, V_DIM, D_MODEL), V_DIM),
        "w_gate": nrm(ks[11], (DEPTH, D_MODEL, N_BRANCHES * D_MODEL), D_MODEL),
        "b_gate": 0.01 * jax.random.normal(ks[12], (DEPTH, N_BRANCHES * D_MODEL), f32),
        "w_out": nrm(ks[13], (DEPTH, D_MODEL, D_MODEL), D_MODEL),
        "g_ffn": gain(ks[14], (DEPTH, D_MODEL)),
        "w_up": nrm(ks[15], (DEPTH, D_MODEL, D_FF), D_MODEL),
        "w_down": nrm(ks[16], (DEPTH, D_FF, D_MODEL), D_FF),
    }


def reference(x, g_mix, w_in, w_pool_grp, pool_scale, g_q, g_k, lambda_qk, g_sub,
              w_branch_pool, w_branch_attn, w_gate, b_gate, w_out, g_ffn, w_up, w_down):
    B, S, _ = x.shape
    for l in range(DEPTH):
        lambda_init = 0.8 - 0.6 * math.exp(-0.3 * l)
        h = rmsnorm(x, g_mix[l])
        z = h @ w_in[l]
        u_pool = z[..., :POOL_DIM]
        q = z[..., POOL_DIM:POOL_DIM + QK_DIM].reshape(B, S, N_HEADS, 2, HEAD_DIM)
        k = z[..., POOL_DIM + QK_DIM:POOL_DIM + 2 * QK_DIM].reshape(B, S, N_HEADS, 2, HEAD_DIM)
        v = z[..., POOL_DIM + 2 * QK_DIM:].reshape(B, S, N_HEADS, V_HEAD_DIM)

        y_pool = pool_mixer(u_pool, w_pool_grp[l], pool_scale[l])
        lq = lambda_qk[l].astype(jnp.float32)
        lam = jnp.exp(jnp.sum(lq[0] * lq[1])) - jnp.exp(jnp.sum(lq[2] * lq[3])) + lambda_init
        y_attn = diff_attention(q, k, v, g_q[l], g_k[l], lam, g_sub[l], lambda_init)

        gates = jax.nn.sigmoid(h @ w_gate[l] + b_gate[l])
        merged = (gates[..., :D_MODEL] * (y_pool @ w_branch_pool[l])
                  + gates[..., D_MODEL:] * (y_attn @ w_branch_attn[l]))
        x = x + merged @ w_out[l]

        h2 = rmsnorm(x, g_ffn[l])
        x = x + jnp.square(jax.nn.relu(h2 @ w_up[l])) @ w_down[l]
    return x
```

```python
import math
from contextlib import ExitStack

import numpy as np
import ml_dtypes

import concourse.bass as bass
import concourse.mybir as mybir
from concourse.bass_utils import run_bass_kernel_spmd

F32 = mybir.dt.float32
BF16 = mybir.dt.bfloat16
AF = mybir.ActivationFunctionType
ALU = mybir.AluOpType
AX = mybir.AxisListType

D = 1024
NH = 8
POOL_WINDOWS = (2, 4, 8, 16)
IN_DIM = 3584
DFF = 4096
EPS = 1e-6
NEG = -30000.0
FAR = 100.0


class Tok:
    __slots__ = ("name", "writers", "readers", "prev")

    def __init__(self, name=""):
        self.name = name
        self.writers = {}
        self.readers = {}
        self.prev = {}

    def new_gen(self):
        p = {}
        _merge(p, self.writers)
        _merge(p, self.readers)
        self.prev = p
        self.writers = {}
        self.readers = {}


def _merge(dst, src):
    for k, v in src.items():
        if dst.get(k, 0) < v:
            dst[k] = v


class Sched:
    def __init__(self, nc, es):
        self.nc = nc
        self.es = es
        self.engs = {"pe": nc.tensor, "act": nc.scalar, "dve": nc.vector, "pool": nc.gpsimd, "sp": nc.sync}
        self.sems = {}
        self.count = {}
        self.waited = {}
        for e in ("pe", "act", "dve", "pool"):
            self.sems["e:" + e] = es.enter_context(nc.semaphore("sem_" + e))
            self.count["e:" + e] = 0

    def _deps(self, reads, writes, pwrites):
        deps = {}
        for t in reads:
            _merge(deps, t.writers)
        for t in writes:
            t.new_gen()
            _merge(deps, t.prev)
        for t in pwrites:
            _merge(deps, t.prev)
        return deps

    def _emit_waits(self, eng, deps):
        e = self.engs[eng]
        for k, v in deps.items():
            if self.waited.get((eng, k), 0) < v:
                e.wait_ge(self.sems[k], v)
                self.waited[(eng, k)] = v

    def _record(self, ev, reads, writes, pwrites):
        k, v = ev
        for t in reads:
            if t.readers.get(k, 0) < v:
                t.readers[k] = v
        for t in list(writes) + list(pwrites):
            if t.writers.get(k, 0) < v:
                t.writers[k] = v

    def op(self, eng, fn, reads=(), writes=(), pwrites=()):
        deps = self._deps(reads, writes, pwrites)
        self._emit_waits(eng, deps)
        inst = fn()
        k = "e:" + eng
        self.count[k] += 1
        inst.then_inc(self.sems[k], 1)
        self._record((k, self.count[k]), reads, writes, pwrites)

    def dma(self, queue, key, out, in_, reads=(), writes=(), pwrites=()):
        k = "d:" + key
        if k not in self.sems:
            self.sems[k] = self.es.enter_context(self.nc.semaphore("sem_" + key))
            self.count[k] = 0
        deps = self._deps(reads, writes, pwrites)
        self._emit_waits(queue, deps)
        self.count[k] += 16
        self.engs[queue].dma_start(out=out, in_=in_).then_inc(self.sems[k], 16)
        self._record((k, self.count[k]), reads, writes, pwrites)

    def barrier(self, engines=("pe", "act", "dve", "sp", "pool")):
        deps = {k: v for k, v in self.count.items() if v > 0}
        for e in engines:
            self._emit_waits(e, deps)

    def final_wait(self, eng="sp"):
        deps = {k: v for k, v in self.count.items() if v > 0}
        self._emit_waits(eng, deps)


def build_program(S=4096, L=2, layer0=0, debug=False):
    assert S % 512 == 0
    NT = S // 128
    NST = S // 512
    NKB = S // 128
    nc = bass.Bass("TRN2", target_bir_lowering=False)

    def din(name, shape, dt=F32):
        return nc.dram_tensor(name, list(shape), dt, kind="ExternalInput").ap()

    def dscr(name, shape, dt=BF16, out=False):
        return nc.dram_tensor(name, list(shape), dt, kind="ExternalOutput" if (out and debug) else "Internal").ap()

    x_in = din("x", [S, D])
    w_in = din("w_in", [L, D, IN_DIM])
    w_grp = din("w_pool_grp", [L, 512, 128])
    w_bp = din("w_branch_pool", [L, 512, D])
    w_ba = din("w_branch_attn", [L, D, D])
    w_gate = din("w_gate", [L, D, 2 * D])
    w_out = din("w_out", [L, D, D])
    w_up = din("w_up", [L, D, DFF])
    w_down = din("w_down", [L, DFF, D])
    gcols = din("gcols", [L, 128, 16])
    bgate = din("bgate", [L, 128, 16])
    pscale = din("pscale", [L, 128, 4])
    gqk = din("gqk", [L, 128, 2])
    gsub = din("gsub", [L, 128, 1])
    lamb = din("lamb", [L, 128, 256])
    ident_d = din("ident", [128, 128], BF16)
    qconst_d = din("qconst", [4, S], BF16)
    kconst_d = din("kconst", [NH, 4, S], BF16)
    diag_d = din("diagT", [128, NH, 128], BF16)
    poolrc_d = din("poolrc", [128, 4, 16])
    y_out = nc.dram_tensor("y", [S, D], F32, kind="ExternalOutput").ap()

    qT_d = dscr("qT_s", [NH, 128, S], out=True)
    kT_d = dscr("kT_s", [NH, 128, S], out=True)
    v_d = dscr("v_s", [S, D], out=True)
    ypT_d = dscr("ypT_s", [4, 128, S], out=True)
    yaT_d = dscr("yaT_s", [NH, 128, S], out=True)
    h2T_d = dscr("h2T_s", [8, 128, S], out=True)
    xmid_d = dscr("xmid_s", [S, D], F32, out=True)
    xs_d = dscr("xs_s", [S, D], F32)

    es = ExitStack()
    with es:
        sc = Sched(nc, es)

        uid = [0]

        def sb(name, shape, dt, stack=es):
            uid[0] += 1
            return stack.enter_context(nc.sbuf_tensor(f"sb{uid[0]}_{name}", list(shape), dt))

        def ps(name, shape, dt, stack):
            uid[0] += 1
            return stack.enter_context(nc.psum_tensor(f"ps{uid[0]}_{name}", list(shape), dt))

        ident = sb("ident", [128, 128], BF16)
        onesb = sb("onesb", [128, 128], BF16)
        onesf = sb("onesf", [128, 128], F32)
        diagT = sb("diagT", [128, NH, 128], BF16)
        poolrc = sb("poolrc", [128, 4, 16], F32)
        gcols_sb = sb("gcols", [128, L, 16], F32)
        bgate_sb = sb("bgate", [128, L, 16], F32)
        pscale_sb = sb("pscale", [128, L, 4], F32)
        gqk_sb = sb("gqk", [128, L, 2], F32)
        gsub_sb = sb("gsub", [128, L], F32)
        lam_sb = sb("lam", [128, L, 256], F32)
        lamp = sb("lamp", [128, 2, 64], F32)
        lams = sb("lams", [128, 2], F32)
        lame = sb("lame", [128, 2], F32)
        neglam = sb("neglam", [128, L], F32)
        t_const = Tok("const")
        sc.dma("sp", "const", ident[:], ident_d, pwrites=[t_const])
        sc.dma("sp", "const", diagT[:], diag_d, pwrites=[t_const])
        sc.dma("sp", "const", poolrc[:], poolrc_d, pwrites=[t_const])
        for l in range(L):
            sc.dma("sp", "const", gcols_sb[:, l, :], gcols[l], pwrites=[t_const])
            sc.dma("sp", "const", bgate_sb[:, l, :], bgate[l], pwrites=[t_const])
            sc.dma("sp", "const", pscale_sb[:, l, :], pscale[l], pwrites=[t_const])
            sc.dma("sp", "const", gqk_sb[:, l, :], gqk[l], pwrites=[t_const])
            sc.dma("sp", "const", gsub_sb[:, l:l + 1], gsub[l], pwrites=[t_const])
            sc.dma("sp", "const", lam_sb[:, l, :], lamb[l], pwrites=[t_const])
        t_ones = Tok("ones")
        eps_t = sb("eps_t", [128, 1], F32)
        sc.op("dve", lambda: nc.vector.memset(eps_t[:], EPS), pwrites=[t_ones])
        sc.op("dve", lambda: nc.vector.memset(onesb[:], 1.0), pwrites=[t_ones])
        sc.op("dve", lambda: nc.vector.memset(onesf[:], 1.0 / 128.0), pwrites=[t_ones])
        t_lam = Tok("lam")
        t_lamtmp = Tok("lamtmp")
        for l in range(L):
            lam_init = 0.8 - 0.6 * math.exp(-0.3 * (l + layer0))
            lv = lam_sb[:, l, :].rearrange("p (a d) -> p a d", d=64)
            sc.op("dve", lambda: nc.vector.tensor_tensor(out=lamp[:, 0, :], in0=lv[:, 0, :], in1=lv[:, 1, :], op=ALU.mult),
                  reads=[t_const], writes=[t_lamtmp])
            sc.op("dve", lambda: nc.vector.tensor_tensor(out=lamp[:, 1, :], in0=lv[:, 2, :], in1=lv[:, 3, :], op=ALU.mult),
                  reads=[t_const], pwrites=[t_lamtmp])
            t2 = Tok()
            sc.op("dve", lambda: nc.vector.tensor_reduce(out=lams[:], in_=lamp[:], axis=AX.X, op=ALU.add),
                  reads=[t_lamtmp], writes=[t2])
            t3 = Tok()
            sc.op("act", lambda: nc.scalar.activation(out=lame[:], in_=lams[:], func=AF.Exp), reads=[t2], writes=[t3])
            t4 = Tok()
            sc.op("dve", lambda: nc.vector.tensor_tensor(out=lams[:, 0:1], in0=lame[:, 1:2], in1=lame[:, 0:1], op=ALU.subtract),
                  reads=[t3, t2], writes=[t4])
            sc.op("dve", lambda: nc.vector.tensor_scalar(out=neglam[:, l:l + 1], in0=lams[:, 0:1], scalar1=-lam_init, scalar2=None,
                                                         op0=ALU.add),
                  reads=[t4], pwrites=[t_lam])
            sc.op("dve", lambda: nc.vector.tensor_scalar(out=gsub_sb[:, l:l + 1], in0=gsub_sb[:, l:l + 1], scalar1=1.0 - lam_init,
                                                         scalar2=None, op0=ALU.mult),
                  reads=[t_const, t4], pwrites=[t_lam])
            t_lamtmp = Tok("lamtmp")
            sc.op("dve", lambda: nc.vector.tensor_scalar(out=gqk_sb[:, l, 0:1], in0=gqk_sb[:, l, 0:1], scalar1=0.125, scalar2=None,
                                                         op0=ALU.mult),
                  reads=[t_const], pwrites=[t_lam])

        for l in range(L):
            x_src = x_in if l == 0 else xs_d
            x_dst = y_out if l == L - 1 else xs_d
            t_xsrc = Tok("xsrc")

            with ExitStack() as p1:
                Win = sb("Win", [128, 8, IN_DIM], BF16, p1)
                wgrp = sb("wgrp", [128, 4, 128], BF16, p1)
                t_Win = Tok("Win")
                for kc in range(8):
                    for hf in range(2):
                        sc.dma("pool", "w_a", Win[:, kc, hf * 1792:(hf + 1) * 1792],
                               w_in[l, kc * 128:(kc + 1) * 128, hf * 1792:(hf + 1) * 1792], pwrites=[t_Win])
                sc.dma("pool", "w_a", wgrp[:], w_grp[l].rearrange("(g c) d -> c g d", c=128), pwrites=[t_Win])
                NXS = 3
                xt = [sb(f"p1_x{i}", [128, D], F32, p1) for i in range(NXS)]
                t_xt = [Tok() for _ in range(NXS)]
                junk = sb("p1_junk", [128, D], BF16, p1)
                t_junk = Tok()
                st4 = [sb(f"p1_st{i}", [128, 4], F32, p1) for i in range(NXS)]
                t_st = [[Tok() for _ in range(4)] for _ in range(NXS)]
                hbf = [sb(f"p1_hbf{i}", [128, D], BF16, p1) for i in range(2)]
                t_hbf = [Tok() for _ in range(2)]
                hT = [sb(f"p1_hT{i}", [128, 8, 512], BF16, p1) for i in range(2)]
                t_hT = [[Tok() for _ in range(4)] for _ in range(2)]
                hT_ps = ps("p1_hTps", [128, 8, 128], BF16, p1)
                t_hTps = Tok()
                NZ = 3
                z_ps = [ps(f"p1_zps{i}", [128, 512], F32, p1) for i in range(NZ)]
                t_zps = [Tok() for _ in range(NZ)]
                qkT_ps = [ps(f"p1_qkTps{i}", [128, 8, 128], BF16, p1) for i in range(2)]
                t_qkTps = [Tok() for _ in range(2)]
                u_ps = ps("p1_ups", [128, 512], F32, p1)
                t_ups = Tok()
                y_ps = ps("p1_yps", [128, 512], F32, p1)
                t_yps = Tok()
                zs = [sb(f"p1_zs{i}", [128, 32, 64], F32, p1) for i in range(2)]
                t_zs = [Tok() for _ in range(2)]
                sq = sb("p1_sq", [128, 32, 64], F32, p1)
                t_sq = Tok()
                st32 = sb("p1_st32", [128, 4, 32], F32, p1)
                t_st32 = [Tok() for _ in range(4)]
                qnb = [sb(f"p1_qnb{i}", [128, 32, 64], BF16, p1) for i in range(2)]
                t_qnb = [Tok() for _ in range(2)]
                qk_stage2 = [sb(f"p1_qkst{i}", [128, 2, NH, 512], BF16, p1) for i in range(2)]
                t_qkst2 = [Tok() for _ in range(2)]
                vrow = [sb(f"p1_vrow{i}", [128, D], BF16, p1) for i in range(2)]
                t_vrow = [Tok() for _ in range(2)]
                ubuf = sb("p1_ubuf", [128, 4, 528], F32, p1)
                t_ubuf = [Tok() for _ in range(4)]
                pa = [sb(f"p1_pa{i}", [128, 528], F32, p1) for i in range(2)]
                t_pa = [Tok() for _ in range(2)]
                mixed = sb("p1_mixed", [128, 4, 512], BF16, p1)
                t_mixed = [Tok() for _ in range(4)]
                tmp16 = sb("p1_tmp16", [128, 16], F32, p1)
                t_tmp16 = Tok()
                yp_stage = sb("p1_ypst", [128, 4, 512], BF16, p1)
                t_ypst = Tok()
                t_qTd, t_kTd, t_vd, t_ypd = Tok(), Tok(), Tok(), Tok()
                gmix_b = gcols_sb[:, l, 0:8].unsqueeze(2).to_broadcast([128, 8, 128])

                for g in range(4):
                    sc.op("dve", lambda: nc.vector.memset(ubuf[:, g, 0:16], 0.0), writes=[t_ubuf[g]])

                def p1_A(i):
                    xs = i % NXS
                    hs = i % 2
                    hts = (i // 4) % 2
                    c = i % 4
                    sc.dma("sp", f"p1x{xs}", xt[xs][:], x_src[i * 128:(i + 1) * 128, :], reads=[t_xsrc], writes=[t_xt[xs]])
                    sc.op("act", lambda: nc.scalar.activation(out=junk[:], in_=xt[xs][:], func=AF.Square, accum_out=st4[xs][:, 0:1]),
                          reads=[t_xt[xs]], writes=[t_junk, t_st[xs][0]])
                    sc.op("dve", lambda: nc.vector.tensor_scalar(out=st4[xs][:, 1:2], in0=st4[xs][:, 0:1], scalar1=1.0 / D, scalar2=EPS,
                                                                 op0=ALU.mult, op1=ALU.add),
                          reads=[t_st[xs][0]], writes=[t_st[xs][1]])
                    sc.op("act", lambda: nc.scalar.activation(out=st4[xs][:, 2:3], in_=st4[xs][:, 1:2], func=AF.Sqrt),
                          reads=[t_st[xs][1]], writes=[t_st[xs][2]])
                    sc.op("dve", lambda: nc.vector.reciprocal(out=st4[xs][:, 3:4], in_=st4[xs][:, 2:3]),
                          reads=[t_st[xs][2]], writes=[t_st[xs][3]])
                    sc.op("act", lambda: nc.scalar.activation(out=hbf[hs][:], in_=xt[xs][:], func=AF.Copy, scale=st4[xs][:, 3:4]),
                          reads=[t_xt[xs], t_st[xs][3]], writes=[t_hbf[hs]])

                def p1_AT(i):
                    hs = i % 2
                    hts = (i // 4) % 2
                    c = i % 4

                    def tr():
                        for kc in range(8):
                            ins = nc.tensor.transpose(hT_ps[:, kc, :], hbf[hs][:, kc * 128:(kc + 1) * 128], ident[:])
                        return ins
                    sc.op("pe", tr, reads=[t_hbf[hs], t_const], writes=[t_hTps])
                    sc.op("dve", lambda: nc.vector.tensor_tensor(out=hT[hts][:, :, c * 128:(c + 1) * 128], in0=hT_ps[:], in1=gmix_b,
                                                                 op=ALU.mult),
                          reads=[t_hTps, t_const], writes=[t_hT[hts][c]])

                def p1_Bmm(i):
                    hts = (i // 4) % 2
                    c = i % 4
                    vs = i % 2
                    zsl = i % 2
                    for ct in range(6):
                        if ct == 3 and i + 1 < NT:
                            p1_AT(i + 1)
                        zb = (i * 6 + ct) % NZ
                        col0 = 512 + ct * 512

                        def mm():
                            for kc in range(8):
                                ins = nc.tensor.matmul(z_ps[zb][:], lhsT=hT[hts][:, kc, c * 128:(c + 1) * 128],
                                                       rhs=Win[:, kc, col0:col0 + 512], start=(kc == 0), stop=(kc == 7))
                            return ins
                        sc.op("pe", mm, reads=[t_hT[hts][c], t_Win], writes=[t_zps[zb]])
                        if ct < 4:
                            zv = z_ps[zb][:].rearrange("p (g d) -> p g d", d=64)
                            kw = dict(writes=[t_zs[zsl]]) if ct == 0 else dict(pwrites=[t_zs[zsl]])
                            sc.op("act", lambda: nc.scalar.copy(out=zs[zsl][:, ct * 8:(ct + 1) * 8, :], in_=zv), reads=[t_zps[zb]], **kw)
                        else:
                            vc = (ct - 4) * 512
                            kw = dict(writes=[t_vrow[vs]]) if ct == 4 else dict(pwrites=[t_vrow[vs]])
                            sc.op("act", lambda: nc.scalar.copy(out=vrow[vs][:, vc:vc + 512], in_=z_ps[zb][:]), reads=[t_zps[zb]], **kw)
                    sc.dma("sp", f"p1v{vs}", v_d[i * 128:(i + 1) * 128, :], vrow[vs][:], reads=[t_vrow[vs]], pwrites=[t_vd])

                def p1_stats(i):
                    zsl = i % 2
                    sc.op("act", lambda: nc.scalar.activation(out=sq[:], in_=zs[zsl][:], func=AF.Square), reads=[t_zs[zsl]], writes=[t_sq])
                    sc.op("dve", lambda: nc.vector.tensor_reduce(out=st32[:, 0, :], in_=sq[:], axis=AX.X, op=ALU.add),
                          reads=[t_sq], writes=[t_st32[0]])
                    sc.op("dve", lambda: nc.vector.tensor_scalar(out=st32[:, 1, :], in0=st32[:, 0, :], scalar1=1.0 / 64.0, scalar2=EPS,
                                                                 op0=ALU.mult, op1=ALU.add),
                          reads=[t_st32[0]], writes=[t_st32[1]])
                    sc.op("act", lambda: nc.scalar.activation(out=st32[:, 2, :], in_=st32[:, 1, :], func=AF.Sqrt),
                          reads=[t_st32[1]], writes=[t_st32[2]])
                    sc.op("dve", lambda: nc.vector.reciprocal(out=st32[:, 3, :], in_=st32[:, 2, :]), reads=[t_st32[2]], writes=[t_st32[3]])
                    sc.op("dve", lambda: nc.vector.tensor_tensor(out=qnb[zsl][:], in0=zs[zsl][:],
                                                                 in1=st32[:, 3, :].unsqueeze(2).to_broadcast([128, 32, 64]), op=ALU.mult),
                          reads=[t_zs[zsl], t_st32[3]], writes=[t_qnb[zsl]])

                def p1_BT(i):
                    c = i % 4
                    st = i // 4
                    zsl = i % 2
                    qflat = qnb[zsl][:].rearrange("p g d -> p (g d)")
                    qk_stage, t_qkst = qk_stage2[st % 2], t_qkst2[st % 2]
                    if c == 0:
                        t_qkst.new_gen()
                    for which in range(2):
                        def tr2():
                            for j in range(8):
                                ins = nc.tensor.transpose(qkT_ps[which][:, j, :], qflat[:, which * 1024 + j * 128:which * 1024 + (j + 1) * 128],
                                                          ident[:])
                            return ins
                        sc.op("pe", tr2, reads=[t_qnb[zsl], t_const], writes=[t_qkTps[which]])
                        sc.op("act", lambda: nc.scalar.activation(out=qk_stage[:, which, :, c * 128:(c + 1) * 128], in_=qkT_ps[which][:],
                                                                  func=AF.Copy, scale=gqk_sb[:, l, which:which + 1]),
                              reads=[t_qkTps[which], t_lam], pwrites=[t_qkst])
                    if c != 3:
                        return
                    sc.dma("sp", "p1qk", qT_d[:, :, st * 512:(st + 1) * 512].rearrange("h p t -> p h t"), qk_stage[:, 0, :, :],
                           reads=[t_qkst], pwrites=[t_qTd])
                    sc.dma("sp", "p1qk", kT_d[:, :, st * 512:(st + 1) * 512].rearrange("h p t -> p h t"), qk_stage[:, 1, :, :],
                           reads=[t_qkst], pwrites=[t_kTd])

                upool = [u_ps, y_ps]
                t_upool = [t_ups, t_yps]

                def p1_pool_u(st):
                    hts = st % 2
                    for g in range(4):
                        def mmu():
                            for kc in range(8):
                                ins = nc.tensor.matmul(upool[g % 2][:], lhsT=Win[:, kc, g * 128:(g + 1) * 128], rhs=hT[hts][:, kc, :],
                                                       start=(kc == 0), stop=(kc == 7))
                            return ins
                        sc.op("pe", mmu, reads=t_hT[hts] + [t_Win], writes=[t_upool[g % 2]])
                        sc.op("act", lambda: nc.scalar.copy(out=ubuf[:, g, 16:528], in_=upool[g % 2][:]), reads=[t_upool[g % 2]],
                              pwrites=[t_ubuf[g]])

                def p1_pool_dve(st):
                    for g in range(4):
                        w = POOL_WINDOWS[g]
                        src_ap, src_tok = ubuf[:, g, :], t_ubuf[g]
                        sh, k = 1, 0
                        lo = 0
                        while sh < w:
                            dst = pa[k % 2]
                            lo2 = lo + sh
                            s_ap = src_ap
                            sc.op("dve", lambda: nc.vector.tensor_tensor(out=dst[:, lo2:528], in0=s_ap[:, lo2:528],
                                                                         in1=s_ap[:, lo2 - sh:528 - sh], op=ALU.add),
                                  reads=[src_tok], writes=[t_pa[k % 2]])
                            src_ap, src_tok = dst[:], t_pa[k % 2]
                            lo = lo2
                            sh *= 2
                            k += 1
                        acc_ap = src_ap
                        sc.op("dve", lambda: nc.vector.scalar_tensor_tensor(out=mixed[:, g, :], in0=acc_ap[:, 16:528], scalar=1.0 / w,
                                                                            in1=ubuf[:, g, 16:528], op0=ALU.mult, op1=ALU.subtract),
                              reads=[src_tok, t_ubuf[g]], writes=[t_mixed[g]])
                        if st == 0:
                            sc.op("dve", lambda: nc.vector.tensor_tensor(out=tmp16[:], in0=acc_ap[:, 16:32], in1=poolrc[:, g, :], op=ALU.mult),
                                  reads=[src_tok, t_const], writes=[t_tmp16])
                            sc.op("dve", lambda: nc.vector.tensor_tensor(out=mixed[:, g, 0:16], in0=tmp16[:], in1=ubuf[:, g, 16:32],
                                                                         op=ALU.subtract),
                                  reads=[t_tmp16, t_ubuf[g]], pwrites=[t_mixed[g]])
                        sc.op("dve", lambda: nc.vector.tensor_copy(out=ubuf[:, g, 0:16], in_=ubuf[:, g, 512:528]),
                              reads=[t_ubuf[g]], writes=[t_ubuf[g]])

                def p1_pool_y(st):
                    for g in range(4):
                        sc.op("pe", lambda: nc.tensor.matmul(upool[g % 2][:], lhsT=wgrp[:, g, :], rhs=mixed[:, g, :], start=True, stop=True),
                              reads=[t_mixed[g], t_Win], writes=[t_upool[g % 2]])
                        if g == 0:
                            t_ypst.new_gen()
                        sc.op("act", lambda: nc.scalar.activation(out=yp_stage[:, g, :], in_=upool[g % 2][:], func=AF.Copy,
                                                                  scale=pscale_sb[:, l, g:g + 1]),
                              reads=[t_upool[g % 2], t_const], pwrites=[t_ypst])
                    sc.dma("sp", "p1yp", ypT_d[:, :, st * 512:(st + 1) * 512].rearrange("g p t -> p g t"), yp_stage[:],
                           reads=[t_ypst], pwrites=[t_ypd])

                LA = 2
                for i in range(min(LA, NT)):
                    p1_A(i)
                p1_AT(0)
                for i in range(NT):
                    if i + LA < NT:
                        p1_A(i + LA)
                    p1_Bmm(i)
                    if i >= 1:
                        p1_BT(i - 1)
                    p1_stats(i)
                    if i % 4 == 0 and i >= 4:
                        p1_pool_y(i // 4 - 1)
                    if i % 4 == 3:
                        p1_pool_u(i // 4)
                        p1_pool_dve(i // 4)
                p1_BT(NT - 1)
                p1_pool_y(NT // 4 - 1)
                sc.barrier()

            w3 = ExitStack()
            wg = sb("p3_wg", [128, 8, 2 * D], BF16, w3)
            wbp_sb = sb("p3_wbp", [128, 4, D], BF16, w3)
            wba_sb = sb("p3_wba", [128, 8, D], BF16, w3)
            wo = sb("p3_wo", [128, 8, D], BF16, w3)
            t_w3 = Tok()
            for kc in range(8):
                sc.dma("pool", "w_b", wg[:, kc, :], w_gate[l, kc * 128:(kc + 1) * 128, :], pwrites=[t_w3])
            for kc in range(4):
                sc.dma("pool", "w_b", wbp_sb[:, kc, :], w_bp[l, kc * 128:(kc + 1) * 128, :], pwrites=[t_w3])
            for kc in range(0, 8, 2):
                sc.dma("pool", "w_b", wba_sb[:, kc:kc + 2, :], w_ba[l, kc * 128:(kc + 2) * 128, :].rearrange("(k p) n -> p k n", p=128),
                       pwrites=[t_w3])
            for kc in range(0, 8, 2):
                sc.dma("pool", "w_b", wo[:, kc:kc + 2, :], w_out[l, kc * 128:(kc + 2) * 128, :].rearrange("(k p) n -> p k n", p=128),
                       pwrites=[t_w3])
            with ExitStack() as p2:
                qa = [[sb(f"p2_qa{s}{m}", [68, S], BF16, p2) for m in range(2)] for s in range(2)]
                ka = [[sb(f"p2_ka{s}{m}", [68, S], BF16, p2) for m in range(2)] for s in range(2)]
                vh = [sb(f"p2_vh{s}", [128, NKB, 128], BF16, p2) for s in range(2)]
                t_head = [Tok() for _ in range(2)]
                yah = [sb(f"p2_yah{s}", [128, S], BF16, p2) for s in range(2)]
                t_yah = [Tok() for _ in range(2)]
                NP = 4
                P = [sb(f"p2_P{i}", [128, 512], BF16, p2) for i in range(NP)]
                t_P = [Tok() for _ in range(NP)]
                NSB = 4
                sfree = list(range(NSB))
                sbank = {}
                S_ps = [ps(f"p2_S{i}", [128, 512], F32, p2) for i in range(NSB)]
                t_S = [Tok() for _ in range(NSB)]
                O_ps = [ps(f"p2_O{i}", [128, 512], F32, p2) for i in range(2)]
                D_ps = [ps(f"p2_D{i}", [128, 512], F32, p2) for i in range(2)]
                t_OD = [Tok() for _ in range(2)]
                rD = sb("p2_rD", [128, 512], F32, p2)
                t_rD = Tok()
                a0 = sb("p2_a0", [128, 512], F32, p2)
                t_a0 = Tok()
                b1 = sb("p2_b1", [128, 512], F32, p2)
                t_b1 = Tok()
                oo2 = [sb(f"p2_oo{i}", [128, 512], F32, p2) for i in range(2)]
                t_oo2 = [Tok() for _ in range(2)]
                osq2 = [sb(f"p2_osq{i}", [128, 512], F32, p2) for i in range(2)]
                t_osq2 = [Tok() for _ in range(2)]
                lnD = sb("p2_lnD", [128, 512], F32, p2)
                t_lnD = Tok()
                pending = []
                gbi = [0]
                epi_n = [0]
                rs = sb("p2_rs", [128, 512], F32, p2)
                t_rs = Tok()
                t_yad = Tok()

                def p2_load(h):
                    s = h % 2
                    t_head[s].new_gen()
                    for m in range(2):
                        sc.dma("sp", f"p2h{s}", qa[s][m][0:64, :], qT_d[h, m * 64:(m + 1) * 64, :], pwrites=[t_head[s]])
                        sc.dma("sp", f"p2h{s}", qa[s][m][64:68, :], qconst_d, pwrites=[t_head[s]])
                        sc.dma("sp", f"p2h{s}", ka[s][m][0:64, :], kT_d[h, m * 64:(m + 1) * 64, :], pwrites=[t_head[s]])
                        sc.dma("sp", f"p2h{s}", ka[s][m][64:68, :], kconst_d[h], pwrites=[t_head[s]])
                    sc.dma("sp", f"p2h{s}", vh[s][:], v_d.rearrange("(kb p) (h e) -> h p kb e", p=128, e=128)[h], pwrites=[t_head[s]])

                p2_load(0)
                for h in range(NH):
                    s = h % 2
                    slope = 2.0 ** (-8.0 * (h + 1) / NH)
                    if h + 1 < NH:
                        p2_load(h + 1)
                    blocks = []
                    for t in range(NST):
                        kb_lo = max(0, int(math.floor((512 * t - 127 - FAR / slope) / 128.0)) + 1)
                        for m in range(2):
                            nkb = 4 * t + 4
                            for kb in range(kb_lo, nkb):
                                blocks.append((t, m, kb, kb_lo, nkb))

                    def issue_S(bi):
                        t, m, kb, kb_lo, nkb = blocks[bi]
                        j = kb - 4 * t
                        c0 = max(j, 0) * 128
                        sbk = sfree.pop(0)
                        sbank[bi] = sbk

                        def f():
                            ins = nc.tensor.matmul(S_ps[sbk][:, c0:512], lhsT=ka[s][m][0:68, kb * 128:(kb + 1) * 128],
                                                   rhs=qa[s][m][0:68, t * 512 + c0:(t + 1) * 512], start=True, stop=(j < 0))
                            if j >= 0:
                                ins = nc.tensor.matmul(S_ps[sbk][:, c0:c0 + 128], lhsT=ident[:], rhs=diagT[:, h, :], start=False, stop=True)
                            return ins
                        sc.op("pe", f, reads=[t_head[s], t_const], writes=[t_S[sbk]])

                    def epi2(t, h=h, s=s, eb=0):
                        oo, osq, t_oo, t_osq = oo2[eb], osq2[eb], t_oo2[eb], t_osq2[eb]
                        mb = sfree.pop(0)
                        sfree.append(mb)
                        sc.op("pe", lambda: nc.tensor.matmul(S_ps[mb][:], lhsT=onesf[:], rhs=osq[:], start=True, stop=True),
                              reads=[t_osq, t_ones], writes=[t_S[mb]])
                        sc.op("act", lambda: nc.scalar.activation(out=rs[:], in_=S_ps[mb][:], func=AF.Ln, bias=eps_t[:, 0:1]),
                              reads=[t_S[mb], t_ones], writes=[t_rs])
                        sc.op("act", lambda: nc.scalar.activation(out=rs[:], in_=rs[:], func=AF.Exp, scale=-0.5), reads=[t_rs], writes=[t_rs])
                        if t == 0:
                            t_yah[s].new_gen()
                        sc.op("dve", lambda: nc.vector.scalar_tensor_tensor(out=yah[s][:, t * 512:(t + 1) * 512], in0=oo[:],
                                                                            scalar=gsub_sb[:, l:l + 1], in1=rs[:], op0=ALU.mult, op1=ALU.mult),
                              reads=[t_oo, t_rs, t_lam], pwrites=[t_yah[s]])
                        if t == NST - 1:
                            sc.dma("sp", f"p2ya{s}", yaT_d[h], yah[s][:], reads=[t_yah[s]], pwrites=[t_yad])

                    LA2 = 2
                    DEFER = 8
                    for bi in range(min(LA2, len(blocks))):
                        issue_S(bi)
                    for bi, (t, m, kb, kb_lo, nkb) in enumerate(blocks):
                        if bi + LA2 < len(blocks):
                            issue_S(bi + LA2)
                        gbi[0] += 1
                        j = kb - 4 * t
                        c0 = max(j, 0) * 128
                        sbk = sbank.pop(bi)
                        pb = bi % NP
                        ob = m
                        sc.op("act", lambda: nc.scalar.activation(out=P[pb][:, c0:512], in_=S_ps[sbk][:, c0:512], func=AF.Exp),
                              reads=[t_S[sbk]], writes=[t_P[pb]])
                        sfree.append(sbk)
                        while pending and pending[0][0] <= gbi[0]:
                            pending.pop(0)[1]()

                        def av():
                            nc.tensor.matmul(O_ps[ob][:, c0:512], lhsT=vh[s][:, kb, :], rhs=P[pb][:, c0:512], start=(kb == kb_lo),
                                             stop=(kb == nkb - 1))
                            return nc.tensor.matmul(D_ps[ob][:, c0:512], lhsT=onesb[:], rhs=P[pb][:, c0:512], start=(kb == kb_lo),
                                                    stop=(kb == nkb - 1))
                        if kb == kb_lo:
                            sc.op("pe", av, reads=[t_P[pb], t_head[s], t_ones], writes=[t_OD[ob]])
                        else:
                            sc.op("pe", av, reads=[t_P[pb], t_head[s], t_ones], pwrites=[t_OD[ob]])
                        if kb != nkb - 1:
                            continue
                        sc.op("dve", lambda: nc.vector.reciprocal(out=rD[:], in_=D_ps[ob][:]), reads=[t_OD[ob]], writes=[t_rD])
                        if m == 0:
                            sc.op("dve", lambda: nc.vector.tensor_tensor(out=a0[:], in0=O_ps[ob][:], in1=rD[:], op=ALU.mult),
                                  reads=[t_OD[ob], t_rD], writes=[t_a0])
                            continue
                        eb = epi_n[0] % 2
                        epi_n[0] += 1
                        sc.op("dve", lambda: nc.vector.tensor_tensor(out=b1[:], in0=O_ps[ob][:], in1=rD[:], op=ALU.mult),
                              reads=[t_OD[ob], t_rD], writes=[t_b1])
                        sc.op("dve", lambda: nc.vector.scalar_tensor_tensor(out=oo2[eb][:], in0=b1[:], scalar=neglam[:, l:l + 1], in1=a0[:],
                                                                            op0=ALU.mult, op1=ALU.add),
                              reads=[t_b1, t_a0, t_lam], writes=[t_oo2[eb]])
                        sc.op("dve", lambda: nc.vector.tensor_tensor(out=osq2[eb][:], in0=oo2[eb][:], in1=oo2[eb][:], op=ALU.mult),
                              reads=[t_oo2[eb]], writes=[t_osq2[eb]])
                        pending.append((gbi[0] + DEFER, (lambda f, tt, ee: (lambda: f(tt, eb=ee)))(epi2, t, eb)))
                while pending:
                    pending.pop(0)[1]()
                sc.barrier()

            with ExitStack() as p3:
                xs4 = [sb(f"p3_x{i}", [128, 4, D], F32, p3) for i in range(2)]
                t_xs4 = [Tok() for _ in range(2)]
                ypt = [sb(f"p3_ypt{i}", [128, 4, 512], BF16, p3) for i in range(2)]
                yat = [sb(f"p3_yat{i}", [128, 8, 512], BF16, p3) for i in range(2)]
                t_yt = [Tok() for _ in range(2)]
                junk3 = sb("p3_junk", [128, D], BF16, p3)
                t_junk3 = Tok()
                s4 = sb("p3_s4", [128, 4, 4], F32, p3)
                t_s4 = [Tok() for _ in range(4)]
                hbf3 = [sb(f"p3_hbf{i}", [128, D], BF16, p3) for i in range(2)]
                t_hbf3 = [Tok() for _ in range(2)]
                hT3 = sb("p3_hT", [128, 8, 512], BF16, p3)
                t_hT3 = Tok()
                tp_ps = ps("p3_tps", [128, 8, 128], BF16, p3)
                t_tps = Tok()
                g_ps = [ps(f"p3_gps{i}", [128, 512], F32, p3) for i in range(4)]
                t_gps = [Tok() for _ in range(4)]
                o_ps = [ps(f"p3_ops{i}", [128, 512], F32, p3) for i in range(2)]
                t_ops = [Tok() for _ in range(2)]
                gsb = [sb(f"p3_g{i}", [128, 512], F32, p3) for i in range(2)]
                t_gsb = [Tok() for _ in range(2)]
                m0 = sb("p3_m0", [128, 512], F32, p3)
                t_m0 = Tok()
                m1 = sb("p3_m1", [128, 512], F32, p3)
                t_m1 = Tok()
                merged = sb("p3_merged", [128, 8, 512], BF16, p3)
                t_merged = Tok()
                h2st = [sb(f"p3_h2st{i}", [128, 8, 512], BF16, p3) for i in range(1)] * 2
                t_h2st = [Tok()] * 2
                t_xmd, t_h2d = Tok(), Tok()

                def p3_load(st):
                    sl = st % 2
                    sc.dma("sp", f"p3x{sl}", xs4[sl][:], x_src[st * 512:(st + 1) * 512, :].rearrange("(tt p) d -> p tt d", p=128),
                           writes=[t_xs4[sl]])
                    sc.dma("sp", f"p3y{sl}", ypt[sl][:], ypT_d[:, :, st * 512:(st + 1) * 512].rearrange("g p t -> p g t"), writes=[t_yt[sl]])
                    sc.dma("sp", f"p3y{sl}", yat[sl][:], yaT_d[:, :, st * 512:(st + 1) * 512].rearrange("h p t -> p h t"), pwrites=[t_yt[sl]])

                def norm_T(xv_slot, tok_x, dst, t_dst, first_new_gen, gofs):
                    g_b = gcols_sb[:, l, gofs:gofs + 8].unsqueeze(2).to_broadcast([128, 8, 128])
                    for tt in range(4):
                        sc.op("act", lambda: nc.scalar.activation(out=junk3[:], in_=xv_slot[:, tt, :], func=AF.Square,
                                                                  accum_out=s4[:, 0, tt:tt + 1]),
                              reads=[tok_x], writes=[t_junk3] if tt else [t_junk3, t_s4[0]],
                              pwrites=[t_s4[0]] if tt else [])
                    sc.op("dve", lambda: nc.vector.tensor_scalar(out=s4[:, 1, :], in0=s4[:, 0, :], scalar1=1.0 / D, scalar2=EPS,
                                                                 op0=ALU.mult, op1=ALU.add), reads=[t_s4[0]], writes=[t_s4[1]])
                    sc.op("act", lambda: nc.scalar.activation(out=s4[:, 2, :], in_=s4[:, 1, :], func=AF.Sqrt), reads=[t_s4[1]], writes=[t_s4[2]])
                    sc.op("dve", lambda: nc.vector.reciprocal(out=s4[:, 3, :], in_=s4[:, 2, :]), reads=[t_s4[2]], writes=[t_s4[3]])
                    if first_new_gen:
                        t_dst.new_gen()
                    for tt in range(4):
                        hs = tt % 2
                        sc.op("act", lambda: nc.scalar.activation(out=hbf3[hs][:], in_=xv_slot[:, tt, :], func=AF.Copy,
                                                                  scale=s4[:, 3, tt:tt + 1]),
                              reads=[tok_x, t_s4[3]], writes=[t_hbf3[hs]])

                        def tr():
                            for kc in range(8):
                                ins = nc.tensor.transpose(tp_ps[:, kc, :], hbf3[hs][:, kc * 128:(kc + 1) * 128], ident[:])
                            return ins
                        sc.op("pe", tr, reads=[t_hbf3[hs], t_const], writes=[t_tps])
                        sc.op("dve", lambda: nc.vector.tensor_tensor(out=dst[:, :, tt * 128:(tt + 1) * 128], in0=tp_ps[:], in1=g_b, op=ALU.mult),
                              reads=[t_tps, t_const], pwrites=[t_dst])

                p3_load(0)
                for st in range(NST):
                    sl = st % 2
                    if st + 1 < NST:
                        p3_load(st + 1)
                    norm_T(xs4[sl], t_xs4[sl], hT3, t_hT3, True, 0)
                    t_merged.new_gen()
                    for dt in range(8):
                        def mmg(idx, wt, nk, col, rhs_t):
                            def f():
                                for kc in range(nk):
                                    ins = nc.tensor.matmul(g_ps[idx][:], lhsT=wt[:, kc, col:col + 128], rhs=rhs_t[:, kc, :],
                                                           start=(kc == 0), stop=(kc == nk - 1))
                                return ins
                            return f
                        sc.op("pe", mmg(0, wg, 8, dt * 128, hT3), reads=[t_hT3, t_w3], writes=[t_gps[0]])
                        sc.op("pe", mmg(1, wg, 8, D + dt * 128, hT3), reads=[t_hT3, t_w3], writes=[t_gps[1]])
                        sc.op("pe", mmg(2, wbp_sb, 4, dt * 128, ypt[sl]), reads=[t_yt[sl], t_w3], writes=[t_gps[2]])
                        sc.op("pe", mmg(3, wba_sb, 8, dt * 128, yat[sl]), reads=[t_yt[sl], t_w3], writes=[t_gps[3]])
                        sc.op("act", lambda: nc.scalar.activation(out=gsb[0][:], in_=g_ps[0][:], func=AF.Sigmoid,
                                                                  bias=bgate_sb[:, l, dt:dt + 1]),
                              reads=[t_gps[0], t_const], writes=[t_gsb[0]])
                        sc.op("act", lambda: nc.scalar.activation(out=gsb[1][:], in_=g_ps[1][:], func=AF.Sigmoid,
                                                                  bias=bgate_sb[:, l, 8 + dt:9 + dt]),
                              reads=[t_gps[1], t_const], writes=[t_gsb[1]])
                        sc.op("dve", lambda: nc.vector.tensor_tensor(out=m0[:], in0=g_ps[2][:], in1=gsb[0][:], op=ALU.mult),
                              reads=[t_gps[2], t_gsb[0]], writes=[t_m0])
                        sc.op("dve", lambda: nc.vector.tensor_tensor(out=m1[:], in0=g_ps[3][:], in1=gsb[1][:], op=ALU.mult),
                              reads=[t_gps[3], t_gsb[1]], writes=[t_m1])
                        sc.op("dve", lambda: nc.vector.tensor_tensor(out=merged[:, dt, :], in0=m0[:], in1=m1[:], op=ALU.add),
                              reads=[t_m0, t_m1], pwrites=[t_merged])
                    for tt in range(4):
                        for half in range(2):
                            ob = (tt * 2 + half) % 2

                            def mmo():
                                for kc in range(8):
                                    ins = nc.tensor.matmul(o_ps[ob][:], lhsT=merged[:, kc, tt * 128:(tt + 1) * 128],
                                                           rhs=wo[:, kc, half * 512:(half + 1) * 512], start=(kc == 0), stop=(kc == 7))
                                return ins
                            sc.op("pe", mmo, reads=[t_merged, t_w3], writes=[t_ops[ob]])
                            sc.op("dve", lambda: nc.vector.tensor_tensor(out=xs4[sl][:, tt, half * 512:(half + 1) * 512],
                                                                         in0=xs4[sl][:, tt, half * 512:(half + 1) * 512], in1=o_ps[ob][:],
                                                                         op=ALU.add),
                                  reads=[t_ops[ob], t_xs4[sl]], pwrites=[t_xs4[sl]])
                    sc.dma("sp", f"p3xo{sl}", xmid_d[st * 512:(st + 1) * 512, :].rearrange("(tt p) d -> p tt d", p=128), xs4[sl][:],
                           reads=[t_xs4[sl]], pwrites=[t_xmd])
                    norm_T(xs4[sl], t_xs4[sl], h2st[sl], t_h2st[sl], True, 8)
                    sc.dma("sp", f"p3h2{sl}", h2T_d[:, :, st * 512:(st + 1) * 512].rearrange("k p t -> p k t"), h2st[sl][:],
                           reads=[t_h2st[sl]], pwrites=[t_h2d])
                sc.barrier()

            w3.close()

            with ExitStack() as p4:
                TW = 256
                NT2 = S // TW
                wup = sb("p4_wup", [128, 8, DFF], BF16, p4)
                wdn = sb("p4_wdn", [128, 32, D], BF16, p4)
                t_w4 = Tok()
                t_wup = [Tok() for _ in range(4)]
                for cb in range(4):
                    for kc in range(8):
                        sc.dma("pool", f"w4u{cb}", wup[:, kc, cb * 1024:(cb + 1) * 1024],
                               w_up[l, kc * 128:(kc + 1) * 128, cb * 1024:(cb + 1) * 1024], pwrites=[t_wup[cb]])
                for q4 in range(16):
                    sc.dma("pool", "w4d", wdn[:, q4 * 2:(q4 + 1) * 2, :],
                           w_down[l, q4 * 256:(q4 + 1) * 256, :].rearrange("(kc p) n -> p kc n", p=128), pwrites=[t_w4])
                h2t = [sb(f"p4_h2t{i}", [128, 8, TW], BF16, p4) for i in range(2)]
                x2 = [sb(f"p4_x2{i}", [128, 2, D], F32, p4) for i in range(2)]
                t_in4 = [Tok() for _ in range(2)]
                t_x2 = [Tok() for _ in range(2)]
                aT = sb("p4_aT", [128, 32, TW], BF16, p4)
                t_aT = Tok()
                rl = [sb(f"p4_rl{i}", [128, 512], F32, p4) for i in range(2)]
                t_rl = [Tok() for _ in range(2)]
                up_ps = [ps(f"p4_ups{i}", [128, 2, TW], F32, p4) for i in range(3)]
                t_up = [Tok() for _ in range(3)]
                d_ps = [ps(f"p4_dps{i}", [128, 512], F32, p4) for i in range(4)]
                t_dps = [Tok() for _ in range(4)]
                t_xo = Tok()

                def p4_load(i):
                    sl = i % 2
                    sc.dma("sp", f"p4h{sl}", h2t[sl][:], h2T_d[:, :, i * TW:(i + 1) * TW].rearrange("k p t -> p k t"), writes=[t_in4[sl]])
                    sc.dma("sp", f"p4x{sl}", x2[sl][:], xmid_d[i * TW:(i + 1) * TW, :].rearrange("(tt p) d -> p tt d", p=128),
                           writes=[t_x2[sl]])

                p4_load(0)
                for i in range(NT2):
                    sl = i % 2
                    if i + 1 < NT2:
                        p4_load(i + 1)
                    t_aT.new_gen()
                    for fp in range(16):
                        ub = fp % 3
                        rb = fp % 2

                        def mmup():
                            for sub in range(2):
                                ft = fp * 2 + sub
                                for kc in range(8):
                                    ins = nc.tensor.matmul(up_ps[ub][:, sub, :], lhsT=wup[:, kc, ft * 128:(ft + 1) * 128], rhs=h2t[sl][:, kc, :],
                                                           start=(kc == 0), stop=(kc == 7))
                            return ins
                        sc.op("pe", mmup, reads=[t_in4[sl], t_wup[fp // 4]], writes=[t_up[ub]])
                        sc.op("act", lambda: nc.scalar.activation(out=rl[rb][:], in_=up_ps[ub][:].rearrange("p a t -> p (a t)"), func=AF.Relu),
                              reads=[t_up[ub]], writes=[t_rl[rb]])
                        sc.op("dve", lambda: nc.vector.tensor_tensor(out=aT[:, fp * 2:fp * 2 + 2, :].rearrange("p a t -> p (a t)"), in0=rl[rb][:],
                                                                     in1=rl[rb][:], op=ALU.mult),
                              reads=[t_rl[rb]], pwrites=[t_aT])
                    for tt in range(2):
                        for half in range(2):
                            db = tt * 2 + half

                            def mmd():
                                for ft in range(32):
                                    ins = nc.tensor.matmul(d_ps[db][:], lhsT=aT[:, ft, tt * 128:(tt + 1) * 128],
                                                           rhs=wdn[:, ft, half * 512:(half + 1) * 512], start=(ft == 0), stop=(ft == 31))
                                return ins
                            sc.op("pe", mmd, reads=[t_aT, t_w4], writes=[t_dps[db]])
                            sc.op("dve", lambda: nc.vector.tensor_tensor(out=x2[sl][:, tt, half * 512:(half + 1) * 512],
                                                                         in0=x2[sl][:, tt, half * 512:(half + 1) * 512], in1=d_ps[db][:], op=ALU.add),
                                  reads=[t_dps[db], t_x2[sl]], pwrites=[t_x2[sl]])
                    sc.dma("sp", f"p4o{sl}", x_dst[i * TW:(i + 1) * TW, :].rearrange("(tt p) d -> p tt d", p=128), x2[sl][:],
                           reads=[t_x2[sl]], pwrites=[t_xo])
                sc.barrier()
        sc.final_wait("sp")
    return nc


def _constants(S):
    bf = ml_dtypes.bfloat16
    pos = np.arange(S)
    ql = (pos % 128).astype(np.float32)
    qb = (pos // 128).astype(np.float32)
    ones = np.ones(S, np.float32)
    qconst = np.stack([-ql, ones, -128.0 * qb, ones]).astype(bf)
    slopes = np.array([2.0 ** (-8.0 * (i + 1) / NH) for i in range(NH)], np.float32)
    kconst = np.stack([np.stack([sl * ones, sl * ql, sl * ones, sl * 128.0 * qb]) for sl in slopes]).astype(bf)
    kl = np.arange(128)[:, None]
    qq = np.arange(128)[None, :]
    diag = np.zeros((128, NH, 128), np.float32)
    mask = (qq < 64) & (kl >= 64)
    for h in range(NH):
        d = -2.0 * slopes[h] * np.maximum(kl - qq, 0).astype(np.float32)
        diag[:, h, :] = np.where(mask, NEG, d)
    rc = np.zeros((128, 4, 16), np.float32)
    for g, w in enumerate(POOL_WINDOWS):
        rc[:, g, :] = 1.0 / np.minimum(np.arange(16) + 1, w).astype(np.float32)
    ident = np.eye(128, dtype=np.float32).astype(bf)
    return dict(ident=ident, qconst=qconst, kconst=kconst, diagT=diag.astype(bf), poolrc=rc)


def _shared_inputs(S, L, g_mix, w_in, w_pool_grp, pool_scale, g_q, g_k, lambda_qk, g_sub, w_branch_pool, w_branch_attn,
                   w_gate, b_gate, w_out, g_ffn, w_up, w_down):
    f = lambda a: np.ascontiguousarray(np.asarray(a, dtype=np.float32))
    cols = lambda v, n: f(v).reshape(L, n, 128).transpose(0, 2, 1)
    d = dict(
        w_in=f(w_in), w_pool_grp=f(w_pool_grp).reshape(L, 512, 128), w_branch_pool=f(w_branch_pool), w_branch_attn=f(w_branch_attn),
        w_gate=f(w_gate), w_out=f(w_out), w_up=f(w_up), w_down=f(w_down),
        gcols=np.ascontiguousarray(np.concatenate([cols(g_mix, 8), cols(g_ffn, 8)], axis=2)),
        bgate=np.ascontiguousarray(cols(b_gate, 16)),
        pscale=np.ascontiguousarray(cols(pool_scale, 4)),
        gqk=np.ascontiguousarray(np.stack([np.tile(f(g_q), (1, 2)), np.tile(f(g_k), (1, 2))], axis=2)),
        gsub=np.ascontiguousarray(f(g_sub).reshape(L, 128, 1)),
        lamb=np.ascontiguousarray(np.broadcast_to(f(lambda_qk).reshape(L, 1, 256), (L, 128, 256))),
    )
    d.update(_constants(S))
    return d


def kernel(x, g_mix, w_in, w_pool_grp, pool_scale, g_q, g_k, lambda_qk, g_sub, w_branch_pool, w_branch_attn,
           w_gate, b_gate, w_out, g_ffn, w_up, w_down):
    x = np.asarray(x, dtype=np.float32)
    B, S, _ = x.shape
    L = np.asarray(g_mix).shape[0]
    shared = _shared_inputs(S, L, g_mix, w_in, w_pool_grp, pool_scale, g_q, g_k, lambda_qk, g_sub, w_branch_pool,
                            w_branch_attn, w_gate, b_gate, w_out, g_ffn, w_up, w_down)
    nc = build_program(S=S, L=L)
    in_maps = [dict(shared, x=np.ascontiguousarray(x[b])) for b in range(B)]
    res = run_bass_kernel_spmd(nc, in_maps, core_ids=list(range(B)))
    return np.stack([np.asarray(r["y"], dtype=np.float32) for r in res.results], axis=0)
```

```python
import math
from contextlib import ExitStack

import numpy as np
import ml_dtypes

import concourse.bass as bass
import concourse.mybir as mybir
from concourse.bass_utils import run_bass_kernel_spmd

F32 = mybir.dt.float32
BF16 = mybir.dt.bfloat16
AF = mybir.ActivationFunctionType
ALU = mybir.AluOpType
AX = mybir.AxisListType

D = 1024
NH = 8
POOL_WINDOWS = (2, 4, 8, 16)
IN_DIM = 3584
DFF = 4096
EPS = 1e-6
NEG = -30000.0
FAR = 100.0


class Tok:
    __slots__ = ("name", "writers", "readers", "prev")

    def __init__(self, name=""):
        self.name = name
        self.writers = {}
        self.readers = {}
        self.prev = {}

    def new_gen(self):
        p = {}
        _merge(p, self.writers)
        _merge(p, self.readers)
        self.prev = p
        self.writers = {}
        self.readers = {}


def _merge(dst, src):
    for k, v in src.items():
        if dst.get(k, 0) < v:
            dst[k] = v


class Sched:
    def __init__(self, nc, es):
        self.nc = nc
        self.es = es
        self.engs = {"pe": nc.tensor, "act": nc.scalar, "dve": nc.vector, "pool": nc.gpsimd, "sp": nc.sync}
        self.sems = {}
        self.count = {}
        self.waited = {}
        for e in ("pe", "act", "dve", "pool"):
            self.sems["e:" + e] = es.enter_context(nc.semaphore("sem_" + e))
            self.count["e:" + e] = 0

    def _deps(self, reads, writes, pwrites):
        deps = {}
        for t in reads:
            _merge(deps, t.writers)
        for t in writes:
            t.new_gen()
            _merge(deps, t.prev)
        for t in pwrites:
            _merge(deps, t.prev)
        return deps

    def _emit_waits(self, eng, deps):
        e = self.engs[eng]
        for k, v in deps.items():
            if self.waited.get((eng, k), 0) < v:
                e.wait_ge(self.sems[k], v)
                self.waited[(eng, k)] = v

    def _record(self, ev, reads, writes, pwrites):
        k, v = ev
        for t in reads:
            if t.readers.get(k, 0) < v:
                t.readers[k] = v
        for t in list(writes) + list(pwrites):
            if t.writers.get(k, 0) < v:
                t.writers[k] = v

    def op(self, eng, fn, reads=(), writes=(), pwrites=()):
        deps = self._deps(reads, writes, pwrites)
        self._emit_waits(eng, deps)
        inst = fn()
        k = "e:" + eng
        self.count[k] += 1
        inst.then_inc(self.sems[k], 1)
        self._record((k, self.count[k]), reads, writes, pwrites)

    def dma(self, queue, key, out, in_, reads=(), writes=(), pwrites=()):
        k = "d:" + key
        if k not in self.sems:
            self.sems[k] = self.es.enter_context(self.nc.semaphore("sem_" + key))
            self.count[k] = 0
        deps = self._deps(reads, writes, pwrites)
        self._emit_waits(queue, deps)
        self.count[k] += 16
        self.engs[queue].dma_start(out=out, in_=in_).then_inc(self.sems[k], 16)
        self._record((k, self.count[k]), reads, writes, pwrites)

    def barrier(self, engines=("pe", "act", "dve", "sp", "pool")):
        deps = {k: v for k, v in self.count.items() if v > 0}
        for e in engines:
            self._emit_waits(e, deps)

    def final_wait(self, eng="sp"):
        deps = {k: v for k, v in self.count.items() if v > 0}
        self._emit_waits(eng, deps)


def build_program(S=4096, L=2, layer0=0, debug=False):
    assert S % 512 == 0
    NT = S // 128
    NST = S // 512
    NKB = S // 128
    nc = bass.Bass("TRN2", target_bir_lowering=False)

    def din(name, shape, dt=F32):
        return nc.dram_tensor(name, list(shape), dt, kind="ExternalInput").ap()

    def dscr(name, shape, dt=BF16, out=False):
        return nc.dram_tensor(name, list(shape), dt, kind="ExternalOutput" if (out and debug) else "Internal").ap()

    x_in = din("x", [S, D])
    w_in = din("w_in", [L, D, IN_DIM])
    w_grp = din("w_pool_grp", [L, 512, 128])
    w_bp = din("w_branch_pool", [L, 512, D])
    w_ba = din("w_branch_attn", [L, D, D])
    w_gate = din("w_gate", [L, D, 2 * D])
    w_out = din("w_out", [L, D, D])
    w_up = din("w_up", [L, D, DFF])
    w_down = din("w_down", [L, DFF, D])
    gcols = din("gcols", [L, 128, 16])
    bgate = din("bgate", [L, 128, 16])
    pscale = din("pscale", [L, 128, 4])
    gqk = din("gqk", [L, 128, 2])
    gsub = din("gsub", [L, 128, 1])
    lamb = din("lamb", [L, 128, 256])
    ident_d = din("ident", [128, 128], BF16)
    qconst_d = din("qconst", [4, S], BF16)
    kconst_d = din("kconst", [NH, 4, S], BF16)
    diag_d = din("diagT", [128, NH, 128], BF16)
    poolrc_d = din("poolrc", [128, 4, 16])
    y_out = nc.dram_tensor("y", [S, D], F32, kind="ExternalOutput").ap()

    qT_d = dscr("qT_s", [NH, 128, S], out=True)
    kT_d = dscr("kT_s", [NH, 128, S], out=True)
    v_d = dscr("v_s", [S, D], out=True)
    ypT_d = dscr("ypT_s", [4, 128, S], out=True)
    yaT_d = dscr("yaT_s", [NH, 128, S], out=True)
    h2T_d = dscr("h2T_s", [8, 128, S], out=True)
    xmid_d = dscr("xmid_s", [S, D], F32, out=True)
    xs_d = dscr("xs_s", [S, D], F32)

    es = ExitStack()
    with es:
        sc = Sched(nc, es)

        uid = [0]

        def sb(name, shape, dt, stack=es):
            uid[0] += 1
            return stack.enter_context(nc.sbuf_tensor(f"sb{uid[0]}_{name}", list(shape), dt))

        def ps(name, shape, dt, stack):
            uid[0] += 1
            return stack.enter_context(nc.psum_tensor(f"ps{uid[0]}_{name}", list(shape), dt))

        ident = sb("ident", [128, 128], BF16)
        onesb = sb("onesb", [128, 128], BF16)
        onesf = sb("onesf", [128, 128], F32)
        diagT = sb("diagT", [128, NH, 128], BF16)
        poolrc = sb("poolrc", [128, 4, 16], F32)
        gcols_sb = sb("gcols", [128, L, 16], F32)
        bgate_sb = sb("bgate", [128, L, 16], F32)
        pscale_sb = sb("pscale", [128, L, 4], F32)
        gqk_sb = sb("gqk", [128, L, 2], F32)
        gsub_sb = sb("gsub", [128, L], F32)
        lam_sb = sb("lam", [128, L, 256], F32)
        lamp = sb("lamp", [128, 2, 64], F32)
        lams = sb("lams", [128, 2], F32)
        lame = sb("lame", [128, 2], F32)
        neglam = sb("neglam", [128, L], F32)
        t_const = Tok("const")
        sc.dma("sp", "const", ident[:], ident_d, pwrites=[t_const])
        sc.dma("sp", "const", diagT[:], diag_d, pwrites=[t_const])
        sc.dma("sp", "const", poolrc[:], poolrc_d, pwrites=[t_const])
        for l in range(L):
            sc.dma("sp", "const", gcols_sb[:, l, :], gcols[l], pwrites=[t_const])
            sc.dma("sp", "const", bgate_sb[:, l, :], bgate[l], pwrites=[t_const])
            sc.dma("sp", "const", pscale_sb[:, l, :], pscale[l], pwrites=[t_const])
            sc.dma("sp", "const", gqk_sb[:, l, :], gqk[l], pwrites=[t_const])
            sc.dma("sp", "const", gsub_sb[:, l:l + 1], gsub[l], pwrites=[t_const])
            sc.dma("sp", "const", lam_sb[:, l, :], lamb[l], pwrites=[t_const])
        t_ones = Tok("ones")
        eps_t = sb("eps_t", [128, 1], F32)
        sc.op("dve", lambda: nc.vector.memset(eps_t[:], EPS), pwrites=[t_ones])
        sc.op("dve", lambda: nc.vector.memset(onesb[:], 1.0), pwrites=[t_ones])
        sc.op("dve", lambda: nc.vector.memset(onesf[:], 1.0 / 128.0), pwrites=[t_ones])
        t_lam = Tok("lam")
        t_lamtmp = Tok("lamtmp")
        for l in range(L):
            lam_init = 0.8 - 0.6 * math.exp(-0.3 * (l + layer0))
            lv = lam_sb[:, l, :].rearrange("p (a d) -> p a d", d=64)
            sc.op("dve", lambda: nc.vector.tensor_tensor(out=lamp[:, 0, :], in0=lv[:, 0, :], in1=lv[:, 1, :], op=ALU.mult),
                  reads=[t_const], writes=[t_lamtmp])
            sc.op("dve", lambda: nc.vector.tensor_tensor(out=lamp[:, 1, :], in0=lv[:, 2, :], in1=lv[:, 3, :], op=ALU.mult),
                  reads=[t_const], pwrites=[t_lamtmp])
            t2 = Tok()
            sc.op("dve", lambda: nc.vector.tensor_reduce(out=lams[:], in_=lamp[:], axis=AX.X, op=ALU.add),
                  reads=[t_lamtmp], writes=[t2])
            t3 = Tok()
            sc.op("act", lambda: nc.scalar.activation(out=lame[:], in_=lams[:], func=AF.Exp), reads=[t2], writes=[t3])
            t4 = Tok()
            sc.op("dve", lambda: nc.vector.tensor_tensor(out=lams[:, 0:1], in0=lame[:, 1:2], in1=lame[:, 0:1], op=ALU.subtract),
                  reads=[t3, t2], writes=[t4])
            sc.op("dve", lambda: nc.vector.tensor_scalar(out=neglam[:, l:l + 1], in0=lams[:, 0:1], scalar1=-lam_init, scalar2=None,
                                                         op0=ALU.add),
                  reads=[t4], pwrites=[t_lam])
            sc.op("dve", lambda: nc.vector.tensor_scalar(out=gsub_sb[:, l:l + 1], in0=gsub_sb[:, l:l + 1], scalar1=1.0 - lam_init,
                                                         scalar2=None, op0=ALU.mult),
                  reads=[t_const, t4], pwrites=[t_lam])
            t_lamtmp = Tok("lamtmp")
            sc.op("dve", lambda: nc.vector.tensor_scalar(out=gqk_sb[:, l, 0:1], in0=gqk_sb[:, l, 0:1], scalar1=0.125, scalar2=None,
                                                         op0=ALU.mult),
                  reads=[t_const], pwrites=[t_lam])

        for l in range(L):
            x_src = x_in if l == 0 else xs_d
            x_dst = y_out if l == L - 1 else xs_d
            t_xsrc = Tok("xsrc")

            with ExitStack() as p1:
                Win = sb("Win", [128, 8, IN_DIM], BF16, p1)
                wgrp = sb("wgrp", [128, 4, 128], BF16, p1)
                t_Win = Tok("Win")
                for kc in range(8):
                    for hf in range(2):
                        sc.dma("pool", "w_a", Win[:, kc, hf * 1792:(hf + 1) * 1792],
                               w_in[l, kc * 128:(kc + 1) * 128, hf * 1792:(hf + 1) * 1792], pwrites=[t_Win])
                sc.dma("pool", "w_a", wgrp[:], w_grp[l].rearrange("(g c) d -> c g d", c=128), pwrites=[t_Win])
                NXS = 3
                xt = [sb(f"p1_x{i}", [128, D], F32, p1) for i in range(NXS)]
                t_xt = [Tok() for _ in range(NXS)]
                junk = sb("p1_junk", [128, D], BF16, p1)
                t_junk = Tok()
                st4 = [sb(f"p1_st{i}", [128, 4], F32, p1) for i in range(NXS)]
                t_st = [[Tok() for _ in range(4)] for _ in range(NXS)]
                hbf = [sb(f"p1_hbf{i}", [128, D], BF16, p1) for i in range(2)]
                t_hbf = [Tok() for _ in range(2)]
                hT = [sb(f"p1_hT{i}", [128, 8, 512], BF16, p1) for i in range(2)]
                t_hT = [[Tok() for _ in range(4)] for _ in range(2)]
                hT_ps = ps("p1_hTps", [128, 8, 128], BF16, p1)
                t_hTps = Tok()
                NZ = 3
                z_ps = [ps(f"p1_zps{i}", [128, 512], F32, p1) for i in range(NZ)]
                t_zps = [Tok() for _ in range(NZ)]
                qkT_ps = [ps(f"p1_qkTps{i}", [128, 8, 128], BF16, p1) for i in range(2)]
                t_qkTps = [Tok() for _ in range(2)]
                u_ps = ps("p1_ups", [128, 512], F32, p1)
                t_ups = Tok()
                y_ps = ps("p1_yps", [128, 512], F32, p1)
                t_yps = Tok()
                zs = [sb(f"p1_zs{i}", [128, 32, 64], F32, p1) for i in range(2)]
                t_zs = [Tok() for _ in range(2)]
                sq2 = [sb(f"p1_sq{i}", [128, 32, 64], F32, p1) for i in range(2)]
                t_sq2 = [Tok() for _ in range(2)]
                st32 = sb("p1_st32", [128, 4, 32], F32, p1)
                t_st32 = [Tok() for _ in range(4)]
                qnb = [sb(f"p1_qnb{i}", [128, 32, 64], BF16, p1) for i in range(2)]
                t_qnb = [Tok() for _ in range(2)]
                qk_stage2 = [sb(f"p1_qkst{i}", [128, 2, NH, 512], BF16, p1) for i in range(2)]
                t_qkst2 = [Tok() for _ in range(2)]
                vrow = [sb(f"p1_vrow{i}", [128, D], BF16, p1) for i in range(2)]
                t_vrow = [Tok() for _ in range(2)]
                ubuf = sb("p1_ubuf", [128, 4, 528], F32, p1)
                t_ubuf = [Tok() for _ in range(4)]
                pa = [sb(f"p1_pa{i}", [128, 528], F32, p1) for i in range(2)]
                t_pa = [Tok() for _ in range(2)]
                mixed = sb("p1_mixed", [128, 4, 512], BF16, p1)
                t_mixed = [Tok() for _ in range(4)]
                tmp16 = sb("p1_tmp16", [128, 16], F32, p1)
                t_tmp16 = Tok()
                yp_stage = sb("p1_ypst", [128, 4, 512], BF16, p1)
                t_ypst = Tok()
                t_qTd, t_kTd, t_vd, t_ypd = Tok(), Tok(), Tok(), Tok()
                gmix_b = gcols_sb[:, l, 0:8].unsqueeze(2).to_broadcast([128, 8, 128])

                for g in range(4):
                    sc.op("dve", lambda: nc.vector.memset(ubuf[:, g, 0:16], 0.0), writes=[t_ubuf[g]])

                def p1_A(i):
                    xs = i % NXS
                    hs = i % 2
                    hts = (i // 4) % 2
                    c = i % 4
                    sc.dma("sp", f"p1x{xs}", xt[xs][:], x_src[i * 128:(i + 1) * 128, :], reads=[t_xsrc], writes=[t_xt[xs]])
                    sc.op("act", lambda: nc.scalar.activation(out=junk[:], in_=xt[xs][:], func=AF.Square, accum_out=st4[xs][:, 0:1]),
                          reads=[t_xt[xs]], writes=[t_junk, t_st[xs][0]])
                    sc.op("act", lambda: nc.scalar.activation(out=st4[xs][:, 1:2], in_=st4[xs][:, 0:1], func=AF.Ln, scale=1.0 / D,
                                                              bias=eps_t[:, 0:1]),
                          reads=[t_st[xs][0], t_ones], writes=[t_st[xs][1]])
                    sc.op("act", lambda: nc.scalar.activation(out=st4[xs][:, 3:4], in_=st4[xs][:, 1:2], func=AF.Exp, scale=-0.5),
                          reads=[t_st[xs][1]], writes=[t_st[xs][3]])
                    sc.op("act", lambda: nc.scalar.activation(out=hbf[hs][:], in_=xt[xs][:], func=AF.Copy, scale=st4[xs][:, 3:4]),
                          reads=[t_xt[xs], t_st[xs][3]], writes=[t_hbf[hs]])

                def p1_AT(i):
                    hs = i % 2
                    hts = (i // 4) % 2
                    c = i % 4

                    def tr():
                        for kc in range(8):
                            ins = nc.tensor.transpose(hT_ps[:, kc, :], hbf[hs][:, kc * 128:(kc + 1) * 128], ident[:])
                        return ins
                    sc.op("pe", tr, reads=[t_hbf[hs], t_const], writes=[t_hTps])
                    sc.op("dve", lambda: nc.vector.tensor_tensor(out=hT[hts][:, :, c * 128:(c + 1) * 128], in0=hT_ps[:], in1=gmix_b,
                                                                 op=ALU.mult),
                          reads=[t_hTps, t_const], writes=[t_hT[hts][c]])

                def p1_Bmm(i):
                    hts = (i // 4) % 2
                    c = i % 4
                    vs = i % 2
                    zsl = i % 2
                    for ct in range(6):
                        if ct == 3 and i + 1 < NT:
                            p1_AT(i + 1)
                        zb = (i * 6 + ct) % NZ
                        col0 = 512 + ct * 512

                        def mm():
                            for kc in range(8):
                                ins = nc.tensor.matmul(z_ps[zb][:], lhsT=hT[hts][:, kc, c * 128:(c + 1) * 128],
                                                       rhs=Win[:, kc, col0:col0 + 512], start=(kc == 0), stop=(kc == 7))
                            return ins
                        sc.op("pe", mm, reads=[t_hT[hts][c], t_Win], writes=[t_zps[zb]])
                        if ct < 4:
                            zv = z_ps[zb][:].rearrange("p (g d) -> p g d", d=64)
                            kw = dict(writes=[t_zs[zsl]]) if ct == 0 else dict(pwrites=[t_zs[zsl]])
                            sc.op("act", lambda: nc.scalar.copy(out=zs[zsl][:, ct * 8:(ct + 1) * 8, :], in_=zv), reads=[t_zps[zb]], **kw)
                            kw2 = dict(writes=[t_sq2[zsl]]) if ct == 0 else dict(pwrites=[t_sq2[zsl]])
                            sc.op("act", lambda: nc.scalar.activation(out=sq2[zsl][:, ct * 8:(ct + 1) * 8, :], in_=zv, func=AF.Square),
                                  reads=[t_zps[zb]], **kw2)
                        else:
                            vc = (ct - 4) * 512
                            kw = dict(writes=[t_vrow[vs]]) if ct == 4 else dict(pwrites=[t_vrow[vs]])
                            sc.op("act", lambda: nc.scalar.copy(out=vrow[vs][:, vc:vc + 512], in_=z_ps[zb][:]), reads=[t_zps[zb]], **kw)
                    sc.dma("sp", f"p1v{vs}", v_d[i * 128:(i + 1) * 128, :], vrow[vs][:], reads=[t_vrow[vs]], pwrites=[t_vd])

                def p1_stats(i):
                    zsl = i % 2
                    sc.op("dve", lambda: nc.vector.tensor_reduce(out=st32[:, 0, :], in_=sq2[zsl][:], axis=AX.X, op=ALU.add),
                          reads=[t_sq2[zsl]], writes=[t_st32[0]])
                    sc.op("act", lambda: nc.scalar.activation(out=st32[:, 1, :], in_=st32[:, 0, :], func=AF.Ln, scale=1.0 / 64.0,
                                                              bias=eps_t[:, 0:1]),
                          reads=[t_st32[0], t_ones], writes=[t_st32[1]])
                    sc.op("act", lambda: nc.scalar.activation(out=st32[:, 3, :], in_=st32[:, 1, :], func=AF.Exp, scale=-0.5),
                          reads=[t_st32[1]], writes=[t_st32[3]])
                    sc.op("dve", lambda: nc.vector.tensor_tensor(out=qnb[zsl][:], in0=zs[zsl][:],
                                                                 in1=st32[:, 3, :].unsqueeze(2).to_broadcast([128, 32, 64]), op=ALU.mult),
                          reads=[t_zs[zsl], t_st32[3]], writes=[t_qnb[zsl]])

                def p1_BT(i):
                    c = i % 4
                    st = i // 4
                    zsl = i % 2
                    qflat = qnb[zsl][:].rearrange("p g d -> p (g d)")
                    qk_stage, t_qkst = qk_stage2[st % 2], t_qkst2[st % 2]
                    if c == 0:
                        t_qkst.new_gen()
                    for which in range(2):
                        def tr2():
                            for j in range(8):
                                ins = nc.tensor.transpose(qkT_ps[which][:, j, :], qflat[:, which * 1024 + j * 128:which * 1024 + (j + 1) * 128],
                                                          ident[:])
                            return ins
                        sc.op("pe", tr2, reads=[t_qnb[zsl], t_const], writes=[t_qkTps[which]])
                        sc.op("act", lambda: nc.scalar.activation(out=qk_stage[:, which, :, c * 128:(c + 1) * 128], in_=qkT_ps[which][:],
                                                                  func=AF.Copy, scale=gqk_sb[:, l, which:which + 1]),
                              reads=[t_qkTps[which], t_lam], pwrites=[t_qkst])
                    if c != 3:
                        return
                    sc.dma("sp", "p1qk", qT_d[:, :, st * 512:(st + 1) * 512].rearrange("h p t -> p h t"), qk_stage[:, 0, :, :],
                           reads=[t_qkst], pwrites=[t_qTd])
                    sc.dma("sp", "p1qk", kT_d[:, :, st * 512:(st + 1) * 512].rearrange("h p t -> p h t"), qk_stage[:, 1, :, :],
                           reads=[t_qkst], pwrites=[t_kTd])

                upool = [u_ps, y_ps]
                t_upool = [t_ups, t_yps]

                def p1_pool_u(st):
                    hts = st % 2
                    for g in range(4):
                        def mmu():
                            for kc in range(8):
                                ins = nc.tensor.matmul(upool[g % 2][:], lhsT=Win[:, kc, g * 128:(g + 1) * 128], rhs=hT[hts][:, kc, :],
                                                       start=(kc == 0), stop=(kc == 7))
                            return ins
                        sc.op("pe", mmu, reads=t_hT[hts] + [t_Win], writes=[t_upool[g % 2]])
                        sc.op("act", lambda: nc.scalar.copy(out=ubuf[:, g, 16:528], in_=upool[g % 2][:]), reads=[t_upool[g % 2]],
                              pwrites=[t_ubuf[g]])

                def p1_pool_dve(st):
                    for g in range(4):
                        w = POOL_WINDOWS[g]
                        src_ap, src_tok = ubuf[:, g, :], t_ubuf[g]
                        sh, k = 1, 0
                        lo = 0
                        while sh < w:
                            dst = pa[k % 2]
                            lo2 = lo + sh
                            s_ap = src_ap
                            sc.op("dve", lambda: nc.vector.tensor_tensor(out=dst[:, lo2:528], in0=s_ap[:, lo2:528],
                                                                         in1=s_ap[:, lo2 - sh:528 - sh], op=ALU.add),
                                  reads=[src_tok], writes=[t_pa[k % 2]])
                            src_ap, src_tok = dst[:], t_pa[k % 2]
                            lo = lo2
                            sh *= 2
                            k += 1
                        acc_ap = src_ap
                        sc.op("dve", lambda: nc.vector.scalar_tensor_tensor(out=mixed[:, g, :], in0=acc_ap[:, 16:528], scalar=1.0 / w,
                                                                            in1=ubuf[:, g, 16:528], op0=ALU.mult, op1=ALU.subtract),
                              reads=[src_tok, t_ubuf[g]], writes=[t_mixed[g]])
                        if st == 0:
                            sc.op("dve", lambda: nc.vector.tensor_tensor(out=tmp16[:], in0=acc_ap[:, 16:32], in1=poolrc[:, g, :], op=ALU.mult),
                                  reads=[src_tok, t_const], writes=[t_tmp16])
                            sc.op("dve", lambda: nc.vector.tensor_tensor(out=mixed[:, g, 0:16], in0=tmp16[:], in1=ubuf[:, g, 16:32],
                                                                         op=ALU.subtract),
                                  reads=[t_tmp16, t_ubuf[g]], pwrites=[t_mixed[g]])
                        sc.op("dve", lambda: nc.vector.tensor_copy(out=ubuf[:, g, 0:16], in_=ubuf[:, g, 512:528]),
                              reads=[t_ubuf[g]], writes=[t_ubuf[g]])

                def p1_pool_y(st):
                    for g in range(4):
                        sc.op("pe", lambda: nc.tensor.matmul(upool[g % 2][:], lhsT=wgrp[:, g, :], rhs=mixed[:, g, :], start=True, stop=True),
                              reads=[t_mixed[g], t_Win], writes=[t_upool[g % 2]])
                        if g == 0:
                            t_ypst.new_gen()
                        sc.op("act", lambda: nc.scalar.activation(out=yp_stage[:, g, :], in_=upool[g % 2][:], func=AF.Copy,
                                                                  scale=pscale_sb[:, l, g:g + 1]),
                              reads=[t_upool[g % 2], t_const], pwrites=[t_ypst])
                    sc.dma("sp", "p1yp", ypT_d[:, :, st * 512:(st + 1) * 512].rearrange("g p t -> p g t"), yp_stage[:],
                           reads=[t_ypst], pwrites=[t_ypd])

                LA = 2
                for i in range(min(LA, NT)):
                    p1_A(i)
                p1_AT(0)
                for i in range(NT):
                    if i + LA < NT:
                        p1_A(i + LA)
                    p1_Bmm(i)
                    if i >= 1:
                        p1_BT(i - 1)
                    p1_stats(i)
                    if i % 4 == 0 and i >= 4:
                        p1_pool_y(i // 4 - 1)
                    if i % 4 == 3:
                        p1_pool_u(i // 4)
                        p1_pool_dve(i // 4)
                p1_BT(NT - 1)
                p1_pool_y(NT // 4 - 1)
                sc.barrier()

            w3 = ExitStack()
            wg = sb("p3_wg", [128, 8, 2 * D], BF16, w3)
            wbp_sb = sb("p3_wbp", [128, 4, D], BF16, w3)
            wba_sb = sb("p3_wba", [128, 8, D], BF16, w3)
            wo = sb("p3_wo", [128, 8, D], BF16, w3)
            t_w3 = Tok()
            with ExitStack() as p2:
                qa = [[sb(f"p2_qa{s}{m}", [68, S], BF16, p2) for m in range(2)] for s in range(2)]
                ka = [[sb(f"p2_ka{s}{m}", [68, S], BF16, p2) for m in range(2)] for s in range(2)]
                vh = [sb(f"p2_vh{s}", [128, NKB, 128], BF16, p2) for s in range(2)]
                t_head = [Tok() for _ in range(2)]
                yah = [sb(f"p2_yah{s}", [128, S], BF16, p2) for s in range(2)]
                t_yah = [Tok() for _ in range(2)]
                NP = 4
                P = [sb(f"p2_P{i}", [128, 512], BF16, p2) for i in range(NP)]
                t_P = [Tok() for _ in range(NP)]
                NSB = 4
                sfree = list(range(NSB))
                sbank = {}
                S_ps = [ps(f"p2_S{i}", [128, 512], F32, p2) for i in range(NSB)]
                t_S = [Tok() for _ in range(NSB)]
                O_ps = [ps(f"p2_O{i}", [128, 512], F32, p2) for i in range(2)]
                D_ps = [ps(f"p2_D{i}", [128, 512], F32, p2) for i in range(2)]
                t_OD = [Tok() for _ in range(2)]
                rD = sb("p2_rD", [128, 512], F32, p2)
                t_rD = Tok()
                a0 = sb("p2_a0", [128, 512], F32, p2)
                t_a0 = Tok()
                b1 = sb("p2_b1", [128, 512], F32, p2)
                t_b1 = Tok()
                oo2 = [sb(f"p2_oo{i}", [128, 512], F32, p2) for i in range(2)]
                t_oo2 = [Tok() for _ in range(2)]
                osq2 = [sb(f"p2_osq{i}", [128, 512], F32, p2) for i in range(2)]
                t_osq2 = [Tok() for _ in range(2)]
                lnD = sb("p2_lnD", [128, 512], F32, p2)
                t_lnD = Tok()
                pending = []
                gbi = [0]
                epi_n = [0]
                rs = sb("p2_rs", [128, 512], F32, p2)
                t_rs = Tok()
                t_yad = Tok()

                def p2_load(h):
                    s = h % 2
                    t_head[s].new_gen()
                    for m in range(2):
                        sc.dma("sp", f"p2h{s}", qa[s][m][0:64, :], qT_d[h, m * 64:(m + 1) * 64, :], pwrites=[t_head[s]])
                        sc.dma("sp", f"p2h{s}", qa[s][m][64:68, :], qconst_d, pwrites=[t_head[s]])
                        sc.dma("sp", f"p2h{s}", ka[s][m][0:64, :], kT_d[h, m * 64:(m + 1) * 64, :], pwrites=[t_head[s]])
                        sc.dma("sp", f"p2h{s}", ka[s][m][64:68, :], kconst_d[h], pwrites=[t_head[s]])
                    sc.dma("sp", f"p2h{s}", vh[s][:], v_d.rearrange("(kb p) (h e) -> h p kb e", p=128, e=128)[h], pwrites=[t_head[s]])

                def p3_weight_prefetch():
                    for kc in range(8):
                        sc.dma("pool", "w_b", wg[:, kc, :], w_gate[l, kc * 128:(kc + 1) * 128, :], reads=[t_head[0], t_head[1]],
                               pwrites=[t_w3])
                    for kc in range(4):
                        sc.dma("pool", "w_b", wbp_sb[:, kc, :], w_bp[l, kc * 128:(kc + 1) * 128, :], pwrites=[t_w3])
                    for kc in range(0, 8, 2):
                        sc.dma("pool", "w_b", wba_sb[:, kc:kc + 2, :], w_ba[l, kc * 128:(kc + 2) * 128, :].rearrange("(k p) n -> p k n", p=128),
                               pwrites=[t_w3])
                    for kc in range(0, 8, 2):
                        sc.dma("pool", "w_b", wo[:, kc:kc + 2, :], w_out[l, kc * 128:(kc + 2) * 128, :].rearrange("(k p) n -> p k n", p=128),
                               pwrites=[t_w3])

                p2_load(0)
                for h in range(NH):
                    s = h % 2
                    slope = 2.0 ** (-8.0 * (h + 1) / NH)
                    if h + 1 < NH:
                        p2_load(h + 1)
                    if h == 0:
                        p3_weight_prefetch()
                    blocks = []
                    for t in range(NST):
                        kb_lo = max(0, int(math.floor((512 * t - 127 - FAR / slope) / 128.0)) + 1)
                        for m in range(2):
                            nkb = 4 * t + 4
                            for kb in range(kb_lo, nkb):
                                blocks.append((t, m, kb, kb_lo, nkb))

                    def issue_S(bi):
                        t, m, kb, kb_lo, nkb = blocks[bi]
                        j = kb - 4 * t
                        c0 = max(j, 0) * 128
                        sbk = sfree.pop(0)
                        sbank[bi] = sbk

                        def f():
                            ins = nc.tensor.matmul(S_ps[sbk][:, c0:512], lhsT=ka[s][m][0:68, kb * 128:(kb + 1) * 128],
                                                   rhs=qa[s][m][0:68, t * 512 + c0:(t + 1) * 512], start=True, stop=(j < 0))
                            if j >= 0:
                                ins = nc.tensor.matmul(S_ps[sbk][:, c0:c0 + 128], lhsT=ident[:], rhs=diagT[:, h, :], start=False, stop=True)
                            return ins
                        sc.op("pe", f, reads=[t_head[s], t_const], writes=[t_S[sbk]])

                    def epi2(t, h=h, s=s, eb=0):
                        oo, osq, t_oo, t_osq = oo2[eb], osq2[eb], t_oo2[eb], t_osq2[eb]
                        mb = sfree.pop(0)
                        sfree.append(mb)
                        sc.op("pe", lambda: nc.tensor.matmul(S_ps[mb][:], lhsT=onesf[:], rhs=osq[:], start=True, stop=True),
                              reads=[t_osq, t_ones], writes=[t_S[mb]])
                        sc.op("act", lambda: nc.scalar.activation(out=rs[:], in_=S_ps[mb][:], func=AF.Ln, bias=eps_t[:, 0:1]),
                              reads=[t_S[mb], t_ones], writes=[t_rs])
                        sc.op("act", lambda: nc.scalar.activation(out=rs[:], in_=rs[:], func=AF.Exp, scale=-0.5), reads=[t_rs], writes=[t_rs])
                        if t == 0:
                            t_yah[s].new_gen()
                        sc.op("dve", lambda: nc.vector.scalar_tensor_tensor(out=yah[s][:, t * 512:(t + 1) * 512], in0=oo[:],
                                                                            scalar=gsub_sb[:, l:l + 1], in1=rs[:], op0=ALU.mult, op1=ALU.mult),
                              reads=[t_oo, t_rs, t_lam], pwrites=[t_yah[s]])
                        if t == NST - 1:
                            sc.dma("sp", f"p2ya{s}", yaT_d[h], yah[s][:], reads=[t_yah[s]], pwrites=[t_yad])

                    LA2 = 2
                    DEFER = 8
                    for bi in range(min(LA2, len(blocks))):
                        issue_S(bi)
                    for bi, (t, m, kb, kb_lo, nkb) in enumerate(blocks):
                        if bi + LA2 < len(blocks):
                            issue_S(bi + LA2)
                        gbi[0] += 1
                        j = kb - 4 * t
                        c0 = max(j, 0) * 128
                        sbk = sbank.pop(bi)
                        pb = bi % NP
                        ob = m
                        sc.op("act", lambda: nc.scalar.activation(out=P[pb][:, c0:512], in_=S_ps[sbk][:, c0:512], func=AF.Exp),
                              reads=[t_S[sbk]], writes=[t_P[pb]])
                        sfree.append(sbk)
                        while pending and pending[0][0] <= gbi[0]:
                            pending.pop(0)[1]()

                        def av():
                            nc.tensor.matmul(O_ps[ob][:, c0:512], lhsT=vh[s][:, kb, :], rhs=P[pb][:, c0:512], start=(kb == kb_lo),
                                             stop=(kb == nkb - 1))
                            return nc.tensor.matmul(D_ps[ob][:, c0:512], lhsT=onesb[:], rhs=P[pb][:, c0:512], start=(kb == kb_lo),
                                                    stop=(kb == nkb - 1))
                        if kb == kb_lo:
                            sc.op("pe", av, reads=[t_P[pb], t_head[s], t_ones], writes=[t_OD[ob]])
                        else:
                            sc.op("pe", av, reads=[t_P[pb], t_head[s], t_ones], pwrites=[t_OD[ob]])
                        if kb != nkb - 1:
                            continue
                        if m == 0:
                            sc.op("dve", lambda: nc.vector.reciprocal(out=rD[:], in_=D_ps[ob][:]), reads=[t_OD[ob]], writes=[t_rD])
                        else:
                            sc.op("act", lambda: nc.scalar.activation(out=lnD[:], in_=D_ps[ob][:], func=AF.Ln), reads=[t_OD[ob]], writes=[t_lnD])
                            sc.op("act", lambda: nc.scalar.activation(out=rD[:], in_=lnD[:], func=AF.Exp, scale=-1.0), reads=[t_lnD],
                                  writes=[t_rD])
                        if m == 0:
                            sc.op("dve", lambda: nc.vector.tensor_tensor(out=a0[:], in0=O_ps[ob][:], in1=rD[:], op=ALU.mult),
                                  reads=[t_OD[ob], t_rD], writes=[t_a0])
                            continue
                        eb = epi_n[0] % 2
                        epi_n[0] += 1
                        sc.op("dve", lambda: nc.vector.tensor_tensor(out=b1[:], in0=O_ps[ob][:], in1=rD[:], op=ALU.mult),
                              reads=[t_OD[ob], t_rD], writes=[t_b1])
                        sc.op("dve", lambda: nc.vector.scalar_tensor_tensor(out=oo2[eb][:], in0=b1[:], scalar=neglam[:, l:l + 1], in1=a0[:],
                                                                            op0=ALU.mult, op1=ALU.add),
                              reads=[t_b1, t_a0, t_lam], writes=[t_oo2[eb]])
                        sc.op("dve", lambda: nc.vector.tensor_tensor(out=osq2[eb][:], in0=oo2[eb][:], in1=oo2[eb][:], op=ALU.mult),
                              reads=[t_oo2[eb]], writes=[t_osq2[eb]])
                        pending.append((gbi[0] + DEFER, (lambda f, tt, ee: (lambda: f(tt, eb=ee)))(epi2, t, eb)))
                while pending:
                    pending.pop(0)[1]()
                sc.barrier()

            with ExitStack() as p3:
                xs4 = [sb(f"p3_x{i}", [128, 4, D], F32, p3) for i in range(3)]
                t_xs4 = [Tok() for _ in range(3)]
                ypt = [sb(f"p3_ypt{i}", [128, 4, 512], BF16, p3) for i in range(2)]
                yat = [sb(f"p3_yat{i}", [128, 8, 512], BF16, p3) for i in range(2)]
                t_yt = [Tok() for _ in range(2)]
                junk3 = sb("p3_junk", [128, D], BF16, p3)
                t_junk3 = Tok()
                sA = [sb(f"p3_sA{i}", [128, 4, 4], F32, p3) for i in range(2)]
                t_sA = [[Tok() for _ in range(4)] for _ in range(2)]
                sB = [sb(f"p3_sB{i}", [128, 4, 4], F32, p3) for i in range(2)]
                t_sB = [[Tok() for _ in range(4)] for _ in range(2)]
                hbf3 = [sb(f"p3_hbf{i}", [128, D], BF16, p3) for i in range(2)]
                t_hbf3 = [Tok() for _ in range(2)]
                hbn = [0]
                hT3 = [sb(f"p3_hT{i}", [128, 8, 512], BF16, p3) for i in range(2)]
                t_hT3 = [Tok() for _ in range(2)]
                tp_ps = [ps(f"p3_tps{i}", [128, 8, 128], BF16, p3) for i in range(2)]
                t_tps = [Tok() for _ in range(2)]
                g_ps = [ps(f"p3_gps{i}", [128, 512], F32, p3) for i in range(4)]
                t_gps = [Tok() for _ in range(4)]
                o_ps = [ps(f"p3_ops{i}", [128, 512], F32, p3) for i in range(2)]
                t_ops = [Tok() for _ in range(2)]
                gsb = [sb(f"p3_g{i}", [128, 512], F32, p3) for i in range(2)]
                t_gsb = [Tok() for _ in range(2)]
                m0 = sb("p3_m0", [128, 512], F32, p3)
                t_m0 = Tok()
                m1 = sb("p3_m1", [128, 512], F32, p3)
                t_m1 = Tok()
                merged = sb("p3_merged", [128, 8, 512], BF16, p3)
                t_merged = Tok()
                h2st = [sb(f"p3_h2st{i}", [128, 8, 512], BF16, p3) for i in range(2)]
                t_h2st = [Tok() for _ in range(2)]
                t_xmd, t_h2d = Tok(), Tok()

                def stats(xv, tok_x, sbuf_, toks):
                    for tt in range(4):
                        sc.op("act", lambda: nc.scalar.activation(out=junk3[:], in_=xv[:, tt, :], func=AF.Square,
                                                                  accum_out=sbuf_[:, 0, tt:tt + 1]),
                              reads=[tok_x], writes=[t_junk3] if tt else [t_junk3, toks[0]], pwrites=[toks[0]] if tt else [])
                    sc.op("act", lambda: nc.scalar.activation(out=sbuf_[:, 1, :], in_=sbuf_[:, 0, :], func=AF.Ln, scale=1.0 / D,
                                                              bias=eps_t[:, 0:1]), reads=[toks[0], t_ones], writes=[toks[1]])
                    sc.op("act", lambda: nc.scalar.activation(out=sbuf_[:, 3, :], in_=sbuf_[:, 1, :], func=AF.Exp, scale=-0.5),
                          reads=[toks[1]], writes=[toks[3]])

                def norm_T(xv, tok_x, sbuf_, toks, dst, t_dst, gofs):
                    g_b = gcols_sb[:, l, gofs:gofs + 8].unsqueeze(2).to_broadcast([128, 8, 128])
                    t_dst.new_gen()
                    for tt in range(4):
                        hs = hbn[0] % 2
                        hbn[0] += 1
                        sc.op("act", lambda: nc.scalar.activation(out=hbf3[hs][:], in_=xv[:, tt, :], func=AF.Copy, scale=sbuf_[:, 3, tt:tt + 1]),
                              reads=[tok_x, toks[3]], writes=[t_hbf3[hs]])

                        def tr():
                            for kc in range(8):
                                ins = nc.tensor.transpose(tp_ps[hs][:, kc, :], hbf3[hs][:, kc * 128:(kc + 1) * 128], ident[:])
                            return ins
                        sc.op("pe", tr, reads=[t_hbf3[hs], t_const], writes=[t_tps[hs]])
                        sc.op("dve", lambda: nc.vector.tensor_tensor(out=dst[:, :, tt * 128:(tt + 1) * 128], in0=tp_ps[hs][:], in1=g_b, op=ALU.mult),
                              reads=[t_tps[hs], t_const], pwrites=[t_dst])

                def X_load(st):
                    xl, sl = st % 3, st % 2
                    sc.dma("sp", f"p3x{xl}", xs4[xl][:], x_src[st * 512:(st + 1) * 512, :].rearrange("(tt p) d -> p tt d", p=128),
                           writes=[t_xs4[xl]])
                    sc.dma("sp", f"p3y{sl}", ypt[sl][:], ypT_d[:, :, st * 512:(st + 1) * 512].rearrange("g p t -> p g t"), writes=[t_yt[sl]])
                    sc.dma("sp", f"p3y{sl}", yat[sl][:], yaT_d[:, :, st * 512:(st + 1) * 512].rearrange("h p t -> p h t"), pwrites=[t_yt[sl]])

                def X_stats(st):
                    xl, sl = st % 3, st % 2
                    stats(xs4[xl], t_xs4[xl], sA[sl], t_sA[sl])

                def X_T(st):
                    xl, sl = st % 3, st % 2
                    norm_T(xs4[xl], t_xs4[xl], sA[sl], t_sA[sl], hT3[sl], t_hT3[sl], 0)

                def G(st, hooks=()):
                    sl = st % 2
                    t_merged.new_gen()
                    for dt in range(8):
                        for hdt, hfn in hooks:
                            if hdt == dt:
                                hfn()
                        def mmg(idx, wt, nk, col, rhs_t):
                            def f():
                                for kc in range(nk):
                                    ins = nc.tensor.matmul(g_ps[idx][:], lhsT=wt[:, kc, col:col + 128], rhs=rhs_t[:, kc, :],
                                                           start=(kc == 0), stop=(kc == nk - 1))
                                return ins
                            return f
                        sc.op("pe", mmg(0, wg, 8, dt * 128, hT3[sl]), reads=[t_hT3[sl], t_w3], writes=[t_gps[0]])
                        sc.op("pe", mmg(1, wg, 8, D + dt * 128, hT3[sl]), reads=[t_hT3[sl], t_w3], writes=[t_gps[1]])
                        sc.op("pe", mmg(2, wbp_sb, 4, dt * 128, ypt[sl]), reads=[t_yt[sl], t_w3], writes=[t_gps[2]])
                        sc.op("pe", mmg(3, wba_sb, 8, dt * 128, yat[sl]), reads=[t_yt[sl], t_w3], writes=[t_gps[3]])
                        sc.op("act", lambda: nc.scalar.activation(out=gsb[0][:], in_=g_ps[0][:], func=AF.Sigmoid,
                                                                  bias=bgate_sb[:, l, dt:dt + 1]),
                              reads=[t_gps[0], t_const], writes=[t_gsb[0]])
                        sc.op("act", lambda: nc.scalar.activation(out=gsb[1][:], in_=g_ps[1][:], func=AF.Sigmoid,
                                                                  bias=bgate_sb[:, l, 8 + dt:9 + dt]),
                              reads=[t_gps[1], t_const], writes=[t_gsb[1]])
                        sc.op("dve", lambda: nc.vector.tensor_tensor(out=m0[:], in0=g_ps[2][:], in1=gsb[0][:], op=ALU.mult),
                              reads=[t_gps[2], t_gsb[0]], writes=[t_m0])
                        sc.op("dve", lambda: nc.vector.tensor_tensor(out=m1[:], in0=g_ps[3][:], in1=gsb[1][:], op=ALU.mult),
                              reads=[t_gps[3], t_gsb[1]], writes=[t_m1])
                        sc.op("dve", lambda: nc.vector.tensor_tensor(out=merged[:, dt, :], in0=m0[:], in1=m1[:], op=ALU.add),
                              reads=[t_m0, t_m1], pwrites=[t_merged])

                def O(st):
                    xl, sl = st % 3, st % 2
                    for tt in range(4):
                        for half in range(2):
                            ob = (tt * 2 + half) % 2

                            def mmo():
                                for kc in range(8):
                                    ins = nc.tensor.matmul(o_ps[ob][:], lhsT=merged[:, kc, tt * 128:(tt + 1) * 128],
                                                           rhs=wo[:, kc, half * 512:(half + 1) * 512], start=(kc == 0), stop=(kc == 7))
                                return ins
                            sc.op("pe", mmo, reads=[t_merged, t_w3], writes=[t_ops[ob]])
                            sc.op("dve", lambda: nc.vector.tensor_tensor(out=xs4[xl][:, tt, half * 512:(half + 1) * 512],
                                                                         in0=xs4[xl][:, tt, half * 512:(half + 1) * 512], in1=o_ps[ob][:],
                                                                         op=ALU.add),
                                  reads=[t_ops[ob], t_xs4[xl]], pwrites=[t_xs4[xl]])
                    sc.dma("sp", f"p3xo{xl}", xmid_d[st * 512:(st + 1) * 512, :].rearrange("(tt p) d -> p tt d", p=128), xs4[xl][:],
                           reads=[t_xs4[xl]], pwrites=[t_xmd])

                def O_stats(st):
                    xl, sl = st % 3, st % 2
                    stats(xs4[xl], t_xs4[xl], sB[sl], t_sB[sl])

                def T2(st):
                    xl, sl = st % 3, st % 2
                    norm_T(xs4[xl], t_xs4[xl], sB[sl], t_sB[sl], h2st[sl], t_h2st[sl], 8)
                    sc.dma("sp", f"p3h2{sl}", h2T_d[:, :, st * 512:(st + 1) * 512].rearrange("k p t -> p k t"), h2st[sl][:],
                           reads=[t_h2st[sl]], pwrites=[t_h2d])

                X_load(0)
                if NST > 1:
                    X_load(1)
                X_stats(0)
                X_T(0)
                for st in range(NST):
                    hooks = []
                    if st >= 1:
                        hooks.append((1, (lambda a: (lambda: O_stats(a)))(st - 1)))
                    if st + 1 < NST:
                        hooks.append((4, (lambda a: (lambda: X_stats(a)))(st + 1)))
                    G(st, hooks)
                    if st + 1 < NST:
                        X_T(st + 1)
                    if st >= 1:
                        T2(st - 1)
                    if st + 2 < NST:
                        X_load(st + 2)
                    O(st)
                O_stats(NST - 1)
                T2(NST - 1)
                sc.barrier()
            w3.close()

            with ExitStack() as p4:
                TW = 256
                NT2 = S // TW
                wup = sb("p4_wup", [128, 8, DFF], BF16, p4)
                wdn = sb("p4_wdn", [128, 32, D], BF16, p4)
                t_w4 = Tok()
                t_wup = [Tok() for _ in range(4)]
                for cb in range(4):
                    for kc in range(8):
                        sc.dma("pool", f"w4u{cb}", wup[:, kc, cb * 1024:(cb + 1) * 1024],
                               w_up[l, kc * 128:(kc + 1) * 128, cb * 1024:(cb + 1) * 1024], pwrites=[t_wup[cb]])
                for q4 in range(16):
                    sc.dma("pool", "w4d", wdn[:, q4 * 2:(q4 + 1) * 2, :],
                           w_down[l, q4 * 256:(q4 + 1) * 256, :].rearrange("(kc p) n -> p kc n", p=128), pwrites=[t_w4])
                h2t = [sb(f"p4_h2t{i}", [128, 8, TW], BF16, p4) for i in range(2)]
                x2 = [sb(f"p4_x2{i}", [128, 2, D], F32, p4) for i in range(2)]
                t_in4 = [Tok() for _ in range(2)]
                t_x2 = [Tok() for _ in range(2)]
                aT = sb("p4_aT", [128, 32, TW], BF16, p4)
                t_aT = Tok()
                rl = [sb(f"p4_rl{i}", [128, 512], F32, p4) for i in range(2)]
                t_rl = [Tok() for _ in range(2)]
                up_ps = [ps(f"p4_ups{i}", [128, 2, TW], F32, p4) for i in range(3)]
                t_up = [Tok() for _ in range(3)]
                d_ps = [ps(f"p4_dps{i}", [128, 512], F32, p4) for i in range(4)]
                t_dps = [Tok() for _ in range(4)]
                t_xo = Tok()

                def p4_load(i):
                    sl = i % 2
                    sc.dma("sp", f"p4h{sl}", h2t[sl][:], h2T_d[:, :, i * TW:(i + 1) * TW].rearrange("k p t -> p k t"), writes=[t_in4[sl]])
                    sc.dma("sp", f"p4x{sl}", x2[sl][:], xmid_d[i * TW:(i + 1) * TW, :].rearrange("(tt p) d -> p tt d", p=128),
                           writes=[t_x2[sl]])

                p4_load(0)
                for i in range(NT2):
                    sl = i % 2
                    if i + 1 < NT2:
                        p4_load(i + 1)
                    t_aT.new_gen()
                    for fp in range(16):
                        ub = fp % 3
                        rb = fp % 2

                        def mmup():
                            for sub in range(2):
                                ft = fp * 2 + sub
                                for kc in range(8):
                                    ins = nc.tensor.matmul(up_ps[ub][:, sub, :], lhsT=wup[:, kc, ft * 128:(ft + 1) * 128], rhs=h2t[sl][:, kc, :],
                                                           start=(kc == 0), stop=(kc == 7))
                            return ins
                        sc.op("pe", mmup, reads=[t_in4[sl], t_wup[fp // 4]], writes=[t_up[ub]])
                        sc.op("act", lambda: nc.scalar.activation(out=rl[rb][:], in_=up_ps[ub][:].rearrange("p a t -> p (a t)"), func=AF.Relu),
                              reads=[t_up[ub]], writes=[t_rl[rb]])
                        sc.op("dve", lambda: nc.vector.tensor_tensor(out=aT[:, fp * 2:fp * 2 + 2, :].rearrange("p a t -> p (a t)"), in0=rl[rb][:],
                                                                     in1=rl[rb][:], op=ALU.mult),
                              reads=[t_rl[rb]], pwrites=[t_aT])
                    for tt in range(2):
                        for half in range(2):
                            db = tt * 2 + half

                            def mmd():
                                for ft in range(32):
                                    ins = nc.tensor.matmul(d_ps[db][:], lhsT=aT[:, ft, tt * 128:(tt + 1) * 128],
                                                           rhs=wdn[:, ft, half * 512:(half + 1) * 512], start=(ft == 0), stop=(ft == 31))
                                return ins
                            sc.op("pe", mmd, reads=[t_aT, t_w4], writes=[t_dps[db]])
                            sc.op("dve", lambda: nc.vector.tensor_tensor(out=x2[sl][:, tt, half * 512:(half + 1) * 512],
                                                                         in0=x2[sl][:, tt, half * 512:(half + 1) * 512], in1=d_ps[db][:], op=ALU.add),
                                  reads=[t_dps[db], t_x2[sl]], pwrites=[t_x2[sl]])
                    sc.dma("sp", f"p4o{sl}", x_dst[i * TW:(i + 1) * TW, :].rearrange("(tt p) d -> p tt d", p=128), x2[sl][:],
                           reads=[t_x2[sl]], pwrites=[t_xo])
                sc.barrier()
        sc.final_wait("sp")
    return nc


def _constants(S):
    bf = ml_dtypes.bfloat16
    pos = np.arange(S)
    ql = (pos % 128).astype(np.float32)
    qb = (pos // 128).astype(np.float32)
    ones = np.ones(S, np.float32)
    qconst = np.stack([-ql, ones, -128.0 * qb, ones]).astype(bf)
    slopes = np.array([2.0 ** (-8.0 * (i + 1) / NH) for i in range(NH)], np.float32)
    kconst = np.stack([np.stack([sl * ones, sl * ql, sl * ones, sl * 128.0 * qb]) for sl in slopes]).astype(bf)
    kl = np.arange(128)[:, None]
    qq = np.arange(128)[None, :]
    diag = np.zeros((128, NH, 128), np.float32)
    mask = (qq < 64) & (kl >= 64)
    for h in range(NH):
        d = -2.0 * slopes[h] * np.maximum(kl - qq, 0).astype(np.float32)
        diag[:, h, :] = np.where(mask, NEG, d)
    rc = np.zeros((128, 4, 16), np.float32)
    for g, w in enumerate(POOL_WINDOWS):
        rc[:, g, :] = 1.0 / np.minimum(np.arange(16) + 1, w).astype(np.float32)
    ident = np.eye(128, dtype=np.float32).astype(bf)
    return dict(ident=ident, qconst=qconst, kconst=kconst, diagT=diag.astype(bf), poolrc=rc)


def _shared_inputs(S, L, g_mix, w_in, w_pool_grp, pool_scale, g_q, g_k, lambda_qk, g_sub, w_branch_pool, w_branch_attn,
                   w_gate, b_gate, w_out, g_ffn, w_up, w_down):
    f = lambda a: np.ascontiguousarray(np.asarray(a, dtype=np.float32))
    cols = lambda v, n: f(v).reshape(L, n, 128).transpose(0, 2, 1)
    d = dict(
        w_in=f(w_in), w_pool_grp=f(w_pool_grp).reshape(L, 512, 128), w_branch_pool=f(w_branch_pool), w_branch_attn=f(w_branch_attn),
        w_gate=f(w_gate), w_out=f(w_out), w_up=f(w_up), w_down=f(w_down),
        gcols=np.ascontiguousarray(np.concatenate([cols(g_mix, 8), cols(g_ffn, 8)], axis=2)),
        bgate=np.ascontiguousarray(cols(b_gate, 16)),
        pscale=np.ascontiguousarray(cols(pool_scale, 4)),
        gqk=np.ascontiguousarray(np.stack([np.tile(f(g_q), (1, 2)), np.tile(f(g_k), (1, 2))], axis=2)),
        gsub=np.ascontiguousarray(f(g_sub).reshape(L, 128, 1)),
        lamb=np.ascontiguousarray(np.broadcast_to(f(lambda_qk).reshape(L, 1, 256), (L, 128, 256))),
    )
    d.update(_constants(S))
    return d


def kernel(x, g_mix, w_in, w_pool_grp, pool_scale, g_q, g_k, lambda_qk, g_sub, w_branch_pool, w_branch_attn,
           w_gate, b_gate, w_out, g_ffn, w_up, w_down):
    x = np.asarray(x, dtype=np.float32)
    B, S, _ = x.shape
    L = np.asarray(g_mix).shape[0]
    shared = _shared_inputs(S, L, g_mix, w_in, w_pool_grp, pool_scale, g_q, g_k, lambda_qk, g_sub, w_branch_pool,
                            w_branch_attn, w_gate, b_gate, w_out, g_ffn, w_up, w_down)
    nc = build_program(S=S, L=L)
    in_maps = [dict(shared, x=np.ascontiguousarray(x[b])) for b in range(B)]
    res = run_bass_kernel_spmd(nc, in_maps, core_ids=list(range(B)))
    return np.stack([np.asarray(r["y"], dtype=np.float32) for r in res.results], axis=0)
```

```python
import math
from contextlib import ExitStack

import numpy as np
import ml_dtypes

import concourse.bass as bass
import concourse.mybir as mybir
from concourse.bass_utils import run_bass_kernel_spmd

F32 = mybir.dt.float32
BF16 = mybir.dt.bfloat16
AF = mybir.ActivationFunctionType
ALU = mybir.AluOpType
AX = mybir.AxisListType

D = 1024
NH = 8
POOL_WINDOWS = (2, 4, 8, 16)
IN_DIM = 3584
DFF = 4096
EPS = 1e-6
NEG = -30000.0
FAR = 64.0


class Tok:
    __slots__ = ("name", "writers", "readers", "prev")

    def __init__(self, name=""):
        self.name = name
        self.writers = {}
        self.readers = {}
        self.prev = {}

    def new_gen(self):
        p = {}
        _merge(p, self.writers)
        _merge(p, self.readers)
        self.prev = p
        self.writers = {}
        self.readers = {}


def _merge(dst, src):
    for k, v in src.items():
        if dst.get(k, 0) < v:
            dst[k] = v


class Sched:
    def __init__(self, nc, es):
        self.nc = nc
        self.es = es
        self.engs = {"pe": nc.tensor, "act": nc.scalar, "dve": nc.vector, "pool": nc.gpsimd, "sp": nc.sync}
        self.sems = {}
        self.count = {}
        self.waited = {}
        for e in ("pe", "act", "dve", "pool"):
            self.sems["e:" + e] = es.enter_context(nc.semaphore("sem_" + e))
            self.count["e:" + e] = 0

    def _deps(self, reads, writes, pwrites):
        deps = {}
        for t in reads:
            _merge(deps, t.writers)
        for t in writes:
            t.new_gen()
            _merge(deps, t.prev)
        for t in pwrites:
            _merge(deps, t.prev)
        return deps

    def _emit_waits(self, eng, deps):
        e = self.engs[eng]
        for k, v in deps.items():
            if self.waited.get((eng, k), 0) < v:
                e.wait_ge(self.sems[k], v)
                self.waited[(eng, k)] = v

    def _record(self, ev, reads, writes, pwrites):
        k, v = ev
        for t in reads:
            if t.readers.get(k, 0) < v:
                t.readers[k] = v
        for t in list(writes) + list(pwrites):
            if t.writers.get(k, 0) < v:
                t.writers[k] = v

    def op(self, eng, fn, reads=(), writes=(), pwrites=()):
        deps = self._deps(reads, writes, pwrites)
        self._emit_waits(eng, deps)
        inst = fn()
        k = "e:" + eng
        self.count[k] += 1
        inst.then_inc(self.sems[k], 1)
        self._record((k, self.count[k]), reads, writes, pwrites)

    def dma(self, queue, key, out, in_, reads=(), writes=(), pwrites=()):
        k = "d:" + key
        if k not in self.sems:
            self.sems[k] = self.es.enter_context(self.nc.semaphore("sem_" + key))
            self.count[k] = 0
        deps = self._deps(reads, writes, pwrites)
        self._emit_waits(queue, deps)
        self.count[k] += 16
        self.engs[queue].dma_start(out=out, in_=in_).then_inc(self.sems[k], 16)
        self._record((k, self.count[k]), reads, writes, pwrites)

    def barrier(self, engines=("pe", "act", "dve", "sp", "pool")):
        deps = {k: v for k, v in self.count.items() if v > 0}
        for e in engines:
            self._emit_waits(e, deps)

    def final_wait(self, eng="sp"):
        deps = {k: v for k, v in self.count.items() if v > 0}
        self._emit_waits(eng, deps)


def build_program(S=4096, L=2, layer0=0, debug=False):
    assert S % 512 == 0
    NT = S // 128
    NST = S // 512
    NKB = S // 128
    nc = bass.Bass("TRN2", target_bir_lowering=False)

    def din(name, shape, dt=F32):
        return nc.dram_tensor(name, list(shape), dt, kind="ExternalInput").ap()

    def dscr(name, shape, dt=BF16, out=False):
        return nc.dram_tensor(name, list(shape), dt, kind="ExternalOutput" if (out and debug) else "Internal").ap()

    x_in = din("x", [S, D])
    w_in = din("w_in", [L, D, IN_DIM])
    w_grp = din("w_pool_grp", [L, 512, 128])
    w_bp = din("w_branch_pool", [L, 512, D])
    w_ba = din("w_branch_attn", [L, D, D])
    w_gate = din("w_gate", [L, D, 2 * D])
    w_out = din("w_out", [L, D, D])
    w_up = din("w_up", [L, D, DFF])
    w_down = din("w_down", [L, DFF, D])
    gcols = din("gcols", [L, 128, 16])
    bgate = din("bgate", [L, 128, 16])
    pscale = din("pscale", [L, 128, 4])
    gqk = din("gqk", [L, 128, 2])
    gsub = din("gsub", [L, 128, 1])
    lamb = din("lamb", [L, 128, 256])
    ident_d = din("ident", [128, 128], BF16)
    qconst_d = din("qconst", [4, S], BF16)
    kconst_d = din("kconst", [NH, 4, S], BF16)
    diag_d = din("diagT", [128, NH, 128], BF16)
    poolrc_d = din("poolrc", [128, 4, 16])
    y_out = nc.dram_tensor("y", [S, D], F32, kind="ExternalOutput").ap()

    qT_d = dscr("qT_s", [NH, 128, S], out=True)
    kT_d = dscr("kT_s", [NH, 128, S], out=True)
    v_d = dscr("v_s", [S, D], out=True)
    ypT_d = dscr("ypT_s", [4, 128, S], out=True)
    yaT_d = dscr("yaT_s", [NH, 128, S], out=True)
    h2T_d = dscr("h2T_s", [8, 128, S], out=True)
    xmid_d = dscr("xmid_s", [S, D], F32, out=True)
    xs_d = dscr("xs_s", [S, D], F32)

    es = ExitStack()
    with es:
        sc = Sched(nc, es)

        uid = [0]

        def sb(name, shape, dt, stack=es):
            uid[0] += 1
            return stack.enter_context(nc.sbuf_tensor(f"sb{uid[0]}_{name}", list(shape), dt))

        def ps(name, shape, dt, stack):
            uid[0] += 1
            return stack.enter_context(nc.psum_tensor(f"ps{uid[0]}_{name}", list(shape), dt))

        ident = sb("ident", [128, 128], BF16)
        onesb = sb("onesb", [128, 128], BF16)
        onesf = sb("onesf", [128, 128], F32)
        onesm = sb("onesm", [128, 128], BF16)
        diagT = sb("diagT", [128, NH, 128], BF16)
        poolrc = sb("poolrc", [128, 4, 16], F32)
        gcols_sb = sb("gcols", [128, L, 16], F32)
        bgate_sb = sb("bgate", [128, L, 16], F32)
        pscale_sb = sb("pscale", [128, L, 4], F32)
        gqk_sb = sb("gqk", [128, L, 2], F32)
        gsub_sb = sb("gsub", [128, L], F32)
        lam_sb = sb("lam", [128, L, 256], F32)
        lamp = sb("lamp", [128, 2, 64], F32)
        lams = sb("lams", [128, 2], F32)
        lame = sb("lame", [128, 2], F32)
        neglam = sb("neglam", [128, L], F32)
        t_const = Tok("const")
        sc.dma("sp", "const", ident[:], ident_d, pwrites=[t_const])
        sc.dma("sp", "const", diagT[:], diag_d, pwrites=[t_const])
        sc.dma("sp", "const", poolrc[:], poolrc_d, pwrites=[t_const])
        for l in range(L):
            sc.dma("sp", "const", gcols_sb[:, l, :], gcols[l], pwrites=[t_const])
            sc.dma("sp", "const", bgate_sb[:, l, :], bgate[l], pwrites=[t_const])
            sc.dma("sp", "const", pscale_sb[:, l, :], pscale[l], pwrites=[t_const])
            sc.dma("sp", "const", gqk_sb[:, l, :], gqk[l], pwrites=[t_const])
            sc.dma("sp", "const", gsub_sb[:, l:l + 1], gsub[l], pwrites=[t_const])
            sc.dma("sp", "const", lam_sb[:, l, :], lamb[l], pwrites=[t_const])
        t_ones = Tok("ones")
        eps_t = sb("eps_t", [128, 1], F32)
        sc.op("dve", lambda: nc.vector.memset(eps_t[:], EPS), pwrites=[t_ones])
        sc.op("dve", lambda: nc.vector.memset(onesb[:], 1.0), pwrites=[t_ones])
        sc.op("dve", lambda: nc.vector.memset(onesf[:], 1.0 / 128.0), pwrites=[t_ones])
        sc.op("dve", lambda: nc.vector.memset(onesm[:], 1.0 / 128.0), pwrites=[t_ones])
        t_lam = Tok("lam")
        t_lamtmp = Tok("lamtmp")
        for l in range(L):
            lam_init = 0.8 - 0.6 * math.exp(-0.3 * (l + layer0))
            lv = lam_sb[:, l, :].rearrange("p (a d) -> p a d", d=64)
            sc.op("dve", lambda: nc.vector.tensor_tensor(out=lamp[:, 0, :], in0=lv[:, 0, :], in1=lv[:, 1, :], op=ALU.mult),
                  reads=[t_const], writes=[t_lamtmp])
            sc.op("dve", lambda: nc.vector.tensor_tensor(out=lamp[:, 1, :], in0=lv[:, 2, :], in1=lv[:, 3, :], op=ALU.mult),
                  reads=[t_const], pwrites=[t_lamtmp])
            t2 = Tok()
            sc.op("dve", lambda: nc.vector.tensor_reduce(out=lams[:], in_=lamp[:], axis=AX.X, op=ALU.add),
                  reads=[t_lamtmp], writes=[t2])
            t3 = Tok()
            sc.op("act", lambda: nc.scalar.activation(out=lame[:], in_=lams[:], func=AF.Exp), reads=[t2], writes=[t3])
            t4 = Tok()
            sc.op("dve", lambda: nc.vector.tensor_tensor(out=lams[:, 0:1], in0=lame[:, 1:2], in1=lame[:, 0:1], op=ALU.subtract),
                  reads=[t3, t2], writes=[t4])
            sc.op("dve", lambda: nc.vector.tensor_scalar(out=neglam[:, l:l + 1], in0=lams[:, 0:1], scalar1=-lam_init, scalar2=None,
                                                         op0=ALU.add),
                  reads=[t4], pwrites=[t_lam])
            sc.op("dve", lambda: nc.vector.tensor_scalar(out=gsub_sb[:, l:l + 1], in0=gsub_sb[:, l:l + 1], scalar1=1.0 - lam_init,
                                                         scalar2=None, op0=ALU.mult),
                  reads=[t_const, t4], pwrites=[t_lam])
            t_lamtmp = Tok("lamtmp")
            sc.op("dve", lambda: nc.vector.tensor_scalar(out=gqk_sb[:, l, 0:1], in0=gqk_sb[:, l, 0:1], scalar1=0.125, scalar2=None,
                                                         op0=ALU.mult),
                  reads=[t_const], pwrites=[t_lam])

        for l in range(L):
            x_src = x_in if l == 0 else xs_d
            x_dst = y_out if l == L - 1 else xs_d
            t_xsrc = Tok("xsrc")

            with ExitStack() as p1:
                Win = sb("Win", [128, 8, IN_DIM], BF16, p1)
                wgrp = sb("wgrp", [128, 4, 128], BF16, p1)
                t_Win = Tok("Win")
                for kc in range(8):
                    for hf in range(2):
                        sc.dma("pool", "w_a", Win[:, kc, hf * 1792:(hf + 1) * 1792],
                               w_in[l, kc * 128:(kc + 1) * 128, hf * 1792:(hf + 1) * 1792], pwrites=[t_Win])
                sc.dma("pool", "w_a", wgrp[:], w_grp[l].rearrange("(g c) d -> c g d", c=128), pwrites=[t_Win])
                NXS = 3
                xt = [sb(f"p1_x{i}", [128, D], F32, p1) for i in range(NXS)]
                t_xt = [Tok() for _ in range(NXS)]
                junk = sb("p1_junk", [128, D], BF16, p1)
                t_junk = Tok()
                st4 = [sb(f"p1_st{i}", [128, 4], F32, p1) for i in range(NXS)]
                t_st = [[Tok() for _ in range(4)] for _ in range(NXS)]
                hbf = [sb(f"p1_hbf{i}", [128, D], BF16, p1) for i in range(2)]
                t_hbf = [Tok() for _ in range(2)]
                hT = [sb(f"p1_hT{i}", [128, 8, 512], BF16, p1) for i in range(2)]
                t_hT = [[Tok() for _ in range(4)] for _ in range(2)]
                hT_ps = ps("p1_hTps", [128, 8, 128], BF16, p1)
                t_hTps = Tok()
                NZ = 3
                z_ps = [ps(f"p1_zps{i}", [128, 512], F32, p1) for i in range(NZ)]
                t_zps = [Tok() for _ in range(NZ)]
                qkT_ps = [ps(f"p1_qkTps{i}", [128, 8, 128], BF16, p1) for i in range(2)]
                t_qkTps = [Tok() for _ in range(2)]
                u_ps = ps("p1_ups", [128, 512], F32, p1)
                t_ups = Tok()
                y_ps = ps("p1_yps", [128, 512], F32, p1)
                t_yps = Tok()
                zs = [sb(f"p1_zs{i}", [128, 32, 64], F32, p1) for i in range(2)]
                t_zs = [Tok() for _ in range(2)]
                sq2 = [sb(f"p1_sq{i}", [128, 32, 64], F32, p1) for i in range(2)]
                t_sq2 = [Tok() for _ in range(2)]
                st32 = sb("p1_st32", [128, 4, 32], F32, p1)
                t_st32 = [Tok() for _ in range(4)]
                qnb = [sb(f"p1_qnb{i}", [128, 32, 64], BF16, p1) for i in range(2)]
                t_qnb = [Tok() for _ in range(2)]
                qk_stage2 = [sb(f"p1_qkst{i}", [128, 2, NH, 512], BF16, p1) for i in range(2)]
                t_qkst2 = [Tok() for _ in range(2)]
                vrow = [sb(f"p1_vrow{i}", [128, D], BF16, p1) for i in range(2)]
                t_vrow = [Tok() for _ in range(2)]
                ubuf = sb("p1_ubuf", [128, 4, 528], F32, p1)
                t_ubuf = [Tok() for _ in range(4)]
                pa = [sb(f"p1_pa{i}", [128, 528], F32, p1) for i in range(2)]
                t_pa = [Tok() for _ in range(2)]
                mixed = sb("p1_mixed", [128, 4, 512], BF16, p1)
                t_mixed = [Tok() for _ in range(4)]
                tmp16 = sb("p1_tmp16", [128, 16], F32, p1)
                t_tmp16 = Tok()
                yp_stage = sb("p1_ypst", [128, 4, 512], BF16, p1)
                t_ypst = Tok()
                t_qTd, t_kTd, t_vd, t_ypd = Tok(), Tok(), Tok(), Tok()
                gmix_b = gcols_sb[:, l, 0:8].unsqueeze(2).to_broadcast([128, 8, 128])

                for g in range(4):
                    sc.op("dve", lambda: nc.vector.memset(ubuf[:, g, 0:16], 0.0), writes=[t_ubuf[g]])

                def p1_A(i):
                    xs = i % NXS
                    hs = i % 2
                    hts = (i // 4) % 2
                    c = i % 4
                    sc.dma("sp", f"p1x{xs}", xt[xs][:], x_src[i * 128:(i + 1) * 128, :], reads=[t_xsrc], writes=[t_xt[xs]])
                    sc.op("act", lambda: nc.scalar.activation(out=junk[:], in_=xt[xs][:], func=AF.Square, accum_out=st4[xs][:, 0:1]),
                          reads=[t_xt[xs]], writes=[t_junk, t_st[xs][0]])
                    sc.op("act", lambda: nc.scalar.activation(out=st4[xs][:, 1:2], in_=st4[xs][:, 0:1], func=AF.Ln, scale=1.0 / D,
                                                              bias=eps_t[:, 0:1]),
                          reads=[t_st[xs][0], t_ones], writes=[t_st[xs][1]])
                    sc.op("act", lambda: nc.scalar.activation(out=st4[xs][:, 3:4], in_=st4[xs][:, 1:2], func=AF.Exp, scale=-0.5),
                          reads=[t_st[xs][1]], writes=[t_st[xs][3]])
                    sc.op("act", lambda: nc.scalar.activation(out=hbf[hs][:], in_=xt[xs][:], func=AF.Copy, scale=st4[xs][:, 3:4]),
                          reads=[t_xt[xs], t_st[xs][3]], writes=[t_hbf[hs]])

                def p1_AT(i):
                    hs = i % 2
                    hts = (i // 4) % 2
                    c = i % 4

                    def tr():
                        for kc in range(8):
                            ins = nc.tensor.transpose(hT_ps[:, kc, :], hbf[hs][:, kc * 128:(kc + 1) * 128], ident[:])
                        return ins
                    sc.op("pe", tr, reads=[t_hbf[hs], t_const], writes=[t_hTps])
                    sc.op("dve", lambda: nc.vector.tensor_tensor(out=hT[hts][:, :, c * 128:(c + 1) * 128], in0=hT_ps[:], in1=gmix_b,
                                                                 op=ALU.mult),
                          reads=[t_hTps, t_const], writes=[t_hT[hts][c]])

                def p1_Bmm(i):
                    hts = (i // 4) % 2
                    c = i % 4
                    vs = i % 2
                    zsl = i % 2
                    for ct in range(6):
                        if ct == 3 and i + 1 < NT:
                            p1_AT(i + 1)
                        zb = (i * 6 + ct) % NZ
                        col0 = 512 + ct * 512

                        def mm():
                            for kc in range(8):
                                ins = nc.tensor.matmul(z_ps[zb][:], lhsT=hT[hts][:, kc, c * 128:(c + 1) * 128],
                                                       rhs=Win[:, kc, col0:col0 + 512], start=(kc == 0), stop=(kc == 7))
                            return ins
                        sc.op("pe", mm, reads=[t_hT[hts][c], t_Win], writes=[t_zps[zb]])
                        if ct < 4:
                            zv = z_ps[zb][:].rearrange("p (g d) -> p g d", d=64)
                            kw = dict(writes=[t_zs[zsl]]) if ct == 0 else dict(pwrites=[t_zs[zsl]])
                            sc.op("act", lambda: nc.scalar.copy(out=zs[zsl][:, ct * 8:(ct + 1) * 8, :], in_=zv), reads=[t_zps[zb]], **kw)
                            kw2 = dict(writes=[t_sq2[zsl]]) if ct == 0 else dict(pwrites=[t_sq2[zsl]])
                            sc.op("act", lambda: nc.scalar.activation(out=sq2[zsl][:, ct * 8:(ct + 1) * 8, :], in_=zv, func=AF.Square),
                                  reads=[t_zps[zb]], **kw2)
                        else:
                            vc = (ct - 4) * 512
                            kw = dict(writes=[t_vrow[vs]]) if ct == 4 else dict(pwrites=[t_vrow[vs]])
                            sc.op("act", lambda: nc.scalar.copy(out=vrow[vs][:, vc:vc + 512], in_=z_ps[zb][:]), reads=[t_zps[zb]], **kw)
                    sc.dma("sp", f"p1v{vs}", v_d[i * 128:(i + 1) * 128, :], vrow[vs][:], reads=[t_vrow[vs]], pwrites=[t_vd])

                def p1_stats(i):
                    zsl = i % 2
                    sc.op("dve", lambda: nc.vector.tensor_reduce(out=st32[:, 0, :], in_=sq2[zsl][:], axis=AX.X, op=ALU.add),
                          reads=[t_sq2[zsl]], writes=[t_st32[0]])
                    sc.op("act", lambda: nc.scalar.activation(out=st32[:, 1, :], in_=st32[:, 0, :], func=AF.Ln, scale=1.0 / 64.0,
                                                              bias=eps_t[:, 0:1]),
                          reads=[t_st32[0], t_ones], writes=[t_st32[1]])
                    sc.op("act", lambda: nc.scalar.activation(out=st32[:, 3, :], in_=st32[:, 1, :], func=AF.Exp, scale=-0.5),
                          reads=[t_st32[1]], writes=[t_st32[3]])
                    sc.op("dve", lambda: nc.vector.tensor_tensor(out=qnb[zsl][:], in0=zs[zsl][:],
                                                                 in1=st32[:, 3, :].unsqueeze(2).to_broadcast([128, 32, 64]), op=ALU.mult),
                          reads=[t_zs[zsl], t_st32[3]], writes=[t_qnb[zsl]])

                def p1_BT(i):
                    c = i % 4
                    st = i // 4
                    zsl = i % 2
                    qflat = qnb[zsl][:].rearrange("p g d -> p (g d)")
                    qk_stage, t_qkst = qk_stage2[st % 2], t_qkst2[st % 2]
                    if c == 0:
                        t_qkst.new_gen()
                    for which in range(2):
                        def tr2():
                            for j in range(8):
                                ins = nc.tensor.transpose(qkT_ps[which][:, j, :], qflat[:, which * 1024 + j * 128:which * 1024 + (j + 1) * 128],
                                                          ident[:])
                            return ins
                        sc.op("pe", tr2, reads=[t_qnb[zsl], t_const], writes=[t_qkTps[which]])
                        sc.op("act", lambda: nc.scalar.activation(out=qk_stage[:, which, :, c * 128:(c + 1) * 128], in_=qkT_ps[which][:],
                                                                  func=AF.Copy, scale=gqk_sb[:, l, which:which + 1]),
                              reads=[t_qkTps[which], t_lam], pwrites=[t_qkst])
                    if c != 3:
                        return
                    sc.dma("sp", "p1qk", qT_d[:, :, st * 512:(st + 1) * 512].rearrange("h p t -> p h t"), qk_stage[:, 0, :, :],
                           reads=[t_qkst], pwrites=[t_qTd])
                    sc.dma("sp", "p1qk", kT_d[:, :, st * 512:(st + 1) * 512].rearrange("h p t -> p h t"), qk_stage[:, 1, :, :],
                           reads=[t_qkst], pwrites=[t_kTd])

                upool = [u_ps, y_ps]
                t_upool = [t_ups, t_yps]

                def p1_pool_u(st):
                    hts = st % 2
                    for g in range(4):
                        def mmu():
                            for kc in range(8):
                                ins = nc.tensor.matmul(upool[g % 2][:], lhsT=Win[:, kc, g * 128:(g + 1) * 128], rhs=hT[hts][:, kc, :],
                                                       start=(kc == 0), stop=(kc == 7))
                            return ins
                        sc.op("pe", mmu, reads=t_hT[hts] + [t_Win], writes=[t_upool[g % 2]])
                        sc.op("act", lambda: nc.scalar.copy(out=ubuf[:, g, 16:528], in_=upool[g % 2][:]), reads=[t_upool[g % 2]],
                              pwrites=[t_ubuf[g]])

                def p1_pool_dve(st):
                    for g in range(4):
                        w = POOL_WINDOWS[g]
                        src_ap, src_tok = ubuf[:, g, :], t_ubuf[g]
                        sh, k = 1, 0
                        lo = 0
                        while sh < w:
                            dst = pa[k % 2]
                            lo2 = lo + sh
                            s_ap = src_ap
                            sc.op("dve", lambda: nc.vector.tensor_tensor(out=dst[:, lo2:528], in0=s_ap[:, lo2:528],
                                                                         in1=s_ap[:, lo2 - sh:528 - sh], op=ALU.add),
                                  reads=[src_tok], writes=[t_pa[k % 2]])
                            src_ap, src_tok = dst[:], t_pa[k % 2]
                            lo = lo2
                            sh *= 2
                            k += 1
                        acc_ap = src_ap
                        sc.op("dve", lambda: nc.vector.scalar_tensor_tensor(out=mixed[:, g, :], in0=acc_ap[:, 16:528], scalar=1.0 / w,
                                                                            in1=ubuf[:, g, 16:528], op0=ALU.mult, op1=ALU.subtract),
                              reads=[src_tok, t_ubuf[g]], writes=[t_mixed[g]])
                        if st == 0:
                            sc.op("dve", lambda: nc.vector.tensor_tensor(out=tmp16[:], in0=acc_ap[:, 16:32], in1=poolrc[:, g, :], op=ALU.mult),
                                  reads=[src_tok, t_const], writes=[t_tmp16])
                            sc.op("dve", lambda: nc.vector.tensor_tensor(out=mixed[:, g, 0:16], in0=tmp16[:], in1=ubuf[:, g, 16:32],
                                                                         op=ALU.subtract),
                                  reads=[t_tmp16, t_ubuf[g]], pwrites=[t_mixed[g]])
                        sc.op("dve", lambda: nc.vector.tensor_copy(out=ubuf[:, g, 0:16], in_=ubuf[:, g, 512:528]),
                              reads=[t_ubuf[g]], writes=[t_ubuf[g]])

                def p1_pool_y(st):
                    for g in range(4):
                        sc.op("pe", lambda: nc.tensor.matmul(upool[g % 2][:], lhsT=wgrp[:, g, :], rhs=mixed[:, g, :], start=True, stop=True),
                              reads=[t_mixed[g], t_Win], writes=[t_upool[g % 2]])
                        if g == 0:
                            t_ypst.new_gen()
                        sc.op("act", lambda: nc.scalar.activation(out=yp_stage[:, g, :], in_=upool[g % 2][:], func=AF.Copy,
                                                                  scale=pscale_sb[:, l, g:g + 1]),
                              reads=[t_upool[g % 2], t_const], pwrites=[t_ypst])
                    sc.dma("sp", "p1yp", ypT_d[:, :, st * 512:(st + 1) * 512].rearrange("g p t -> p g t"), yp_stage[:],
                           reads=[t_ypst], pwrites=[t_ypd])

                LA = 2
                for i in range(min(LA, NT)):
                    p1_A(i)
                p1_AT(0)
                for i in range(NT):
                    if i + LA < NT:
                        p1_A(i + LA)
                    p1_Bmm(i)
                    if i >= 1:
                        p1_BT(i - 1)
                    p1_stats(i)
                    if i % 4 == 0 and i >= 4:
                        p1_pool_y(i // 4 - 1)
                    if i % 4 == 3:
                        p1_pool_u(i // 4)
                        p1_pool_dve(i // 4)
                p1_BT(NT - 1)
                p1_pool_y(NT // 4 - 1)
                sc.barrier()

            w3 = ExitStack()
            wg = sb("p3_wg", [128, 8, 2 * D], BF16, w3)
            wbp_sb = sb("p3_wbp", [128, 4, D], BF16, w3)
            wba_sb = sb("p3_wba", [128, 8, D], BF16, w3)
            wo = sb("p3_wo", [128, 8, D], BF16, w3)
            t_w3 = Tok()
            with ExitStack() as p2:
                qa = [[sb(f"p2_qa{s}{m}", [68, S], BF16, p2) for m in range(2)] for s in range(2)]
                ka = [[sb(f"p2_ka{s}{m}", [68, S], BF16, p2) for m in range(2)] for s in range(2)]
                vh = [sb(f"p2_vh{s}", [128, NKB, 128], BF16, p2) for s in range(2)]
                t_head = [Tok() for _ in range(2)]
                yah = [sb(f"p2_yah{s}", [128, S], BF16, p2) for s in range(2)]
                t_yah = [Tok() for _ in range(2)]
                NP = 4
                P = [sb(f"p2_P{i}", [128, 512], BF16, p2) for i in range(NP)]
                t_P = [Tok() for _ in range(NP)]
                NSB = 4
                sfree = list(range(NSB))
                sbank = {}
                S_ps = [ps(f"p2_S{i}", [128, 512], F32, p2) for i in range(NSB)]
                t_S = [Tok() for _ in range(NSB)]
                O_ps = [ps(f"p2_O{i}", [128, 512], F32, p2) for i in range(2)]
                D_ps = [ps(f"p2_D{i}", [128, 512], F32, p2) for i in range(2)]
                t_OD = [Tok() for _ in range(2)]
                rD = sb("p2_rD", [128, 512], F32, p2)
                t_rD = Tok()
                a0 = sb("p2_a0", [128, 512], F32, p2)
                t_a0 = Tok()
                b1 = sb("p2_b1", [128, 512], F32, p2)
                t_b1 = Tok()
                oo2 = [sb(f"p2_oo{i}", [128, 512], F32, p2) for i in range(2)]
                t_oo2 = [Tok() for _ in range(2)]
                osq2 = [sb(f"p2_osq{i}", [128, 512], BF16, p2) for i in range(2)]
                t_osq2 = [Tok() for _ in range(2)]
                lnD = sb("p2_lnD", [128, 512], F32, p2)
                t_lnD = Tok()
                pending = []
                gbi = [0]
                epi_n = [0]
                rs = sb("p2_rs", [128, 512], F32, p2)
                t_rs = Tok()
                t_yad = Tok()

                def p2_load(h):
                    s = h % 2
                    t_head[s].new_gen()
                    for m in range(2):
                        sc.dma("sp", f"p2h{s}", qa[s][m][0:64, :], qT_d[h, m * 64:(m + 1) * 64, :], pwrites=[t_head[s]])
                        sc.dma("sp", f"p2h{s}", qa[s][m][64:68, :], qconst_d, pwrites=[t_head[s]])
                        sc.dma("sp", f"p2h{s}", ka[s][m][0:64, :], kT_d[h, m * 64:(m + 1) * 64, :], pwrites=[t_head[s]])
                        sc.dma("sp", f"p2h{s}", ka[s][m][64:68, :], kconst_d[h], pwrites=[t_head[s]])
                    sc.dma("sp", f"p2h{s}", vh[s][:], v_d.rearrange("(kb p) (h e) -> h p kb e", p=128, e=128)[h], pwrites=[t_head[s]])

                def p3_weight_prefetch():
                    for kc in range(8):
                        sc.dma("pool", "w_b", wg[:, kc, :], w_gate[l, kc * 128:(kc + 1) * 128, :], reads=[t_head[0], t_head[1]],
                               pwrites=[t_w3])
                    for kc in range(4):
                        sc.dma("pool", "w_b", wbp_sb[:, kc, :], w_bp[l, kc * 128:(kc + 1) * 128, :], pwrites=[t_w3])
                    for kc in range(0, 8, 2):
                        sc.dma("pool", "w_b", wba_sb[:, kc:kc + 2, :], w_ba[l, kc * 128:(kc + 2) * 128, :].rearrange("(k p) n -> p k n", p=128),
                               pwrites=[t_w3])
                    for kc in range(0, 8, 2):
                        sc.dma("pool", "w_b", wo[:, kc:kc + 2, :], w_out[l, kc * 128:(kc + 2) * 128, :].rearrange("(k p) n -> p k n", p=128),
                               pwrites=[t_w3])

                p2_load(0)
                for h in range(NH):
                    s = h % 2
                    slope = 2.0 ** (-8.0 * (h + 1) / NH)
                    if h + 1 < NH:
                        p2_load(h + 1)
                    if h == 0:
                        p3_weight_prefetch()
                    blocks = []
                    for t in range(NST):
                        kb_lo = max(0, int(math.floor((512 * t - 127 - FAR / slope) / 128.0)) + 1)
                        for m in range(2):
                            nkb = 4 * t + 4
                            for kb in range(kb_lo, nkb):
                                blocks.append((t, m, kb, kb_lo, nkb))

                    def issue_S(bi):
                        t, m, kb, kb_lo, nkb = blocks[bi]
                        j = kb - 4 * t
                        c0 = max(j, 0) * 128
                        sbk = sfree.pop(0)
                        sbank[bi] = sbk

                        def f():
                            ins = nc.tensor.matmul(S_ps[sbk][:, c0:512], lhsT=ka[s][m][0:68, kb * 128:(kb + 1) * 128],
                                                   rhs=qa[s][m][0:68, t * 512 + c0:(t + 1) * 512], start=True, stop=(j < 0))
                            if j >= 0:
                                ins = nc.tensor.matmul(S_ps[sbk][:, c0:c0 + 128], lhsT=ident[:], rhs=diagT[:, h, :], start=False, stop=True)
                            return ins
                        sc.op("pe", f, reads=[t_head[s], t_const], writes=[t_S[sbk]])

                    def epi2(t, h=h, s=s, eb=0):
                        oo, osq, t_oo, t_osq = oo2[eb], osq2[eb], t_oo2[eb], t_osq2[eb]
                        mb = sfree.pop(0)
                        sfree.append(mb)
                        sc.op("pe", lambda: nc.tensor.matmul(S_ps[mb][:], lhsT=onesm[:], rhs=osq[:], start=True, stop=True),
                              reads=[t_osq, t_ones], writes=[t_S[mb]])
                        sc.op("act", lambda: nc.scalar.activation(out=rs[:], in_=S_ps[mb][:], func=AF.Ln, bias=eps_t[:, 0:1]),
                              reads=[t_S[mb], t_ones], writes=[t_rs])
                        sc.op("act", lambda: nc.scalar.activation(out=rs[:], in_=rs[:], func=AF.Exp, scale=-0.5), reads=[t_rs], writes=[t_rs])
                        if t == 0:
                            t_yah[s].new_gen()
                        sc.op("dve", lambda: nc.vector.scalar_tensor_tensor(out=yah[s][:, t * 512:(t + 1) * 512], in0=oo[:],
                                                                            scalar=gsub_sb[:, l:l + 1], in1=rs[:], op0=ALU.mult, op1=ALU.mult),
                              reads=[t_oo, t_rs, t_lam], pwrites=[t_yah[s]])
                        if t == NST - 1:
                            sc.dma("sp", f"p2ya{s}", yaT_d[h], yah[s][:], reads=[t_yah[s]], pwrites=[t_yad])

                    LA2 = 2
                    DEFER = 8
                    for bi in range(min(LA2, len(blocks))):
                        issue_S(bi)
                    for bi, (t, m, kb, kb_lo, nkb) in enumerate(blocks):
                        if bi + LA2 < len(blocks):
                            issue_S(bi + LA2)
                        gbi[0] += 1
                        j = kb - 4 * t
                        c0 = max(j, 0) * 128
                        sbk = sbank.pop(bi)
                        pb = bi % NP
                        ob = m
                        sc.op("act", lambda: nc.scalar.activation(out=P[pb][:, c0:512], in_=S_ps[sbk][:, c0:512], func=AF.Exp),
                              reads=[t_S[sbk]], writes=[t_P[pb]])
                        sfree.append(sbk)
                        while pending and pending[0][0] <= gbi[0]:
                            pending.pop(0)[1]()

                        def av():
                            nc.tensor.matmul(O_ps[ob][:, c0:512], lhsT=vh[s][:, kb, :], rhs=P[pb][:, c0:512], start=(kb == kb_lo),
                                             stop=(kb == nkb - 1))
                            return nc.tensor.matmul(D_ps[ob][:, c0:512], lhsT=onesb[:], rhs=P[pb][:, c0:512], start=(kb == kb_lo),
                                                    stop=(kb == nkb - 1))
                        if kb == kb_lo:
                            sc.op("pe", av, reads=[t_P[pb], t_head[s], t_ones], writes=[t_OD[ob]])
                        else:
                            sc.op("pe", av, reads=[t_P[pb], t_head[s], t_ones], pwrites=[t_OD[ob]])
                        if kb != nkb - 1:
                            continue
                        if m == 0:
                            sc.op("dve", lambda: nc.vector.reciprocal(out=rD[:], in_=D_ps[ob][:]), reads=[t_OD[ob]], writes=[t_rD])
                        else:
                            sc.op("act", lambda: nc.scalar.activation(out=lnD[:], in_=D_ps[ob][:], func=AF.Ln), reads=[t_OD[ob]], writes=[t_lnD])
                            sc.op("act", lambda: nc.scalar.activation(out=rD[:], in_=lnD[:], func=AF.Exp, scale=-1.0), reads=[t_lnD],
                                  writes=[t_rD])
                        if m == 0:
                            sc.op("dve", lambda: nc.vector.tensor_tensor(out=a0[:], in0=O_ps[ob][:], in1=rD[:], op=ALU.mult),
                                  reads=[t_OD[ob], t_rD], writes=[t_a0])
                            continue
                        eb = epi_n[0] % 2
                        epi_n[0] += 1
                        sc.op("dve", lambda: nc.vector.tensor_tensor(out=b1[:], in0=O_ps[ob][:], in1=rD[:], op=ALU.mult),
                              reads=[t_OD[ob], t_rD], writes=[t_b1])
                        sc.op("dve", lambda: nc.vector.scalar_tensor_tensor(out=oo2[eb][:], in0=b1[:], scalar=neglam[:, l:l + 1], in1=a0[:],
                                                                            op0=ALU.mult, op1=ALU.add),
                              reads=[t_b1, t_a0, t_lam], writes=[t_oo2[eb]])
                        sc.op("dve", lambda: nc.vector.tensor_tensor(out=osq2[eb][:], in0=oo2[eb][:], in1=oo2[eb][:], op=ALU.mult),
                              reads=[t_oo2[eb]], writes=[t_osq2[eb]])
                        pending.append((gbi[0] + DEFER, (lambda f, tt, ee: (lambda: f(tt, eb=ee)))(epi2, t, eb)))
                while pending:
                    pending.pop(0)[1]()
                sc.barrier()

            with ExitStack() as p3:
                xs4 = [sb(f"p3_x{i}", [128, 4, D], F32, p3) for i in range(3)]
                t_xs4 = [Tok() for _ in range(3)]
                ypt = [sb(f"p3_ypt{i}", [128, 4, 512], BF16, p3) for i in range(2)]
                yat = [sb(f"p3_yat{i}", [128, 8, 512], BF16, p3) for i in range(2)]
                t_yt = [Tok() for _ in range(2)]
                junk3 = sb("p3_junk", [128, D], BF16, p3)
                t_junk3 = Tok()
                sA = [sb(f"p3_sA{i}", [128, 4, 4], F32, p3) for i in range(2)]
                t_sA = [[Tok() for _ in range(4)] for _ in range(2)]
                sB = [sb(f"p3_sB{i}", [128, 4, 4], F32, p3) for i in range(2)]
                t_sB = [[Tok() for _ in range(4)] for _ in range(2)]
                hbf3 = [sb(f"p3_hbf{i}", [128, D], BF16, p3) for i in range(2)]
                t_hbf3 = [Tok() for _ in range(2)]
                hbn = [0]
                hT3 = [sb(f"p3_hT{i}", [128, 8, 512], BF16, p3) for i in range(2)]
                t_hT3 = [Tok() for _ in range(2)]
                tp_ps = [ps(f"p3_tps{i}", [128, 8, 128], BF16, p3) for i in range(2)]
                t_tps = [Tok() for _ in range(2)]
                g_ps = [ps(f"p3_gps{i}", [128, 512], F32, p3) for i in range(4)]
                t_gps = [Tok() for _ in range(4)]
                o_ps = [ps(f"p3_ops{i}", [128, 512], F32, p3) for i in range(2)]
                t_ops = [Tok() for _ in range(2)]
                gsb = [sb(f"p3_g{i}", [128, 512], F32, p3) for i in range(2)]
                t_gsb = [Tok() for _ in range(2)]
                m0 = sb("p3_m0", [128, 512], F32, p3)
                t_m0 = Tok()
                m1 = sb("p3_m1", [128, 512], F32, p3)
                t_m1 = Tok()
                merged = sb("p3_merged", [128, 8, 512], BF16, p3)
                t_merged = Tok()
                h2st = [sb(f"p3_h2st{i}", [128, 8, 512], BF16, p3) for i in range(2)]
                t_h2st = [Tok() for _ in range(2)]
                t_xmd, t_h2d = Tok(), Tok()

                def stats(xv, tok_x, sbuf_, toks):
                    for tt in range(4):
                        sc.op("act", lambda: nc.scalar.activation(out=junk3[:], in_=xv[:, tt, :], func=AF.Square,
                                                                  accum_out=sbuf_[:, 0, tt:tt + 1]),
                              reads=[tok_x], writes=[t_junk3] if tt else [t_junk3, toks[0]], pwrites=[toks[0]] if tt else [])
                    sc.op("act", lambda: nc.scalar.activation(out=sbuf_[:, 1, :], in_=sbuf_[:, 0, :], func=AF.Ln, scale=1.0 / D,
                                                              bias=eps_t[:, 0:1]), reads=[toks[0], t_ones], writes=[toks[1]])
                    sc.op("act", lambda: nc.scalar.activation(out=sbuf_[:, 3, :], in_=sbuf_[:, 1, :], func=AF.Exp, scale=-0.5),
                          reads=[toks[1]], writes=[toks[3]])

                def norm_T(xv, tok_x, sbuf_, toks, dst, t_dst, gofs):
                    g_b = gcols_sb[:, l, gofs:gofs + 8].unsqueeze(2).to_broadcast([128, 8, 128])
                    t_dst.new_gen()
                    for tt in range(4):
                        hs = hbn[0] % 2
                        hbn[0] += 1
                        sc.op("act", lambda: nc.scalar.activation(out=hbf3[hs][:], in_=xv[:, tt, :], func=AF.Copy, scale=sbuf_[:, 3, tt:tt + 1]),
                              reads=[tok_x, toks[3]], writes=[t_hbf3[hs]])

                        def tr():
                            for kc in range(8):
                                ins = nc.tensor.transpose(tp_ps[hs][:, kc, :], hbf3[hs][:, kc * 128:(kc + 1) * 128], ident[:])
                            return ins
                        sc.op("pe", tr, reads=[t_hbf3[hs], t_const], writes=[t_tps[hs]])
                        sc.op("dve", lambda: nc.vector.tensor_tensor(out=dst[:, :, tt * 128:(tt + 1) * 128], in0=tp_ps[hs][:], in1=g_b, op=ALU.mult),
                              reads=[t_tps[hs], t_const], pwrites=[t_dst])

                def X_load(st):
                    xl, sl = st % 3, st % 2
                    sc.dma("sp", f"p3x{xl}", xs4[xl][:], x_src[st * 512:(st + 1) * 512, :].rearrange("(tt p) d -> p tt d", p=128),
                           writes=[t_xs4[xl]])
                    sc.dma("sp", f"p3y{sl}", ypt[sl][:], ypT_d[:, :, st * 512:(st + 1) * 512].rearrange("g p t -> p g t"), writes=[t_yt[sl]])
                    sc.dma("sp", f"p3y{sl}", yat[sl][:], yaT_d[:, :, st * 512:(st + 1) * 512].rearrange("h p t -> p h t"), pwrites=[t_yt[sl]])

                def X_stats(st):
                    xl, sl = st % 3, st % 2
                    stats(xs4[xl], t_xs4[xl], sA[sl], t_sA[sl])

                def X_T(st):
                    xl, sl = st % 3, st % 2
                    norm_T(xs4[xl], t_xs4[xl], sA[sl], t_sA[sl], hT3[sl], t_hT3[sl], 0)

                def G(st, hooks=()):
                    sl = st % 2
                    t_merged.new_gen()
                    for dt in range(8):
                        for hdt, hfn in hooks:
                            if hdt == dt:
                                hfn()
                        def mmg(idx, wt, nk, col, rhs_t):
                            def f():
                                for kc in range(nk):
                                    ins = nc.tensor.matmul(g_ps[idx][:], lhsT=wt[:, kc, col:col + 128], rhs=rhs_t[:, kc, :],
                                                           start=(kc == 0), stop=(kc == nk - 1))
                                return ins
                            return f
                        sc.op("pe", mmg(0, wg, 8, dt * 128, hT3[sl]), reads=[t_hT3[sl], t_w3], writes=[t_gps[0]])
                        sc.op("pe", mmg(1, wg, 8, D + dt * 128, hT3[sl]), reads=[t_hT3[sl], t_w3], writes=[t_gps[1]])
                        sc.op("pe", mmg(2, wbp_sb, 4, dt * 128, ypt[sl]), reads=[t_yt[sl], t_w3], writes=[t_gps[2]])
                        sc.op("pe", mmg(3, wba_sb, 8, dt * 128, yat[sl]), reads=[t_yt[sl], t_w3], writes=[t_gps[3]])
                        sc.op("act", lambda: nc.scalar.activation(out=gsb[0][:], in_=g_ps[0][:], func=AF.Sigmoid,
                                                                  bias=bgate_sb[:, l, dt:dt + 1]),
                              reads=[t_gps[0], t_const], writes=[t_gsb[0]])
                        sc.op("act", lambda: nc.scalar.activation(out=gsb[1][:], in_=g_ps[1][:], func=AF.Sigmoid,
                                                                  bias=bgate_sb[:, l, 8 + dt:9 + dt]),
                              reads=[t_gps[1], t_const], writes=[t_gsb[1]])
                        sc.op("dve", lambda: nc.vector.tensor_tensor(out=m0[:], in0=g_ps[2][:], in1=gsb[0][:], op=ALU.mult),
                              reads=[t_gps[2], t_gsb[0]], writes=[t_m0])
                        sc.op("dve", lambda: nc.vector.tensor_tensor(out=m1[:], in0=g_ps[3][:], in1=gsb[1][:], op=ALU.mult),
                              reads=[t_gps[3], t_gsb[1]], writes=[t_m1])
                        sc.op("dve", lambda: nc.vector.tensor_tensor(out=merged[:, dt, :], in0=m0[:], in1=m1[:], op=ALU.add),
                              reads=[t_m0, t_m1], pwrites=[t_merged])

                def O(st):
                    xl, sl = st % 3, st % 2
                    for tt in range(4):
                        for half in range(2):
                            ob = (tt * 2 + half) % 2

                            def mmo():
                                for kc in range(8):
                                    ins = nc.tensor.matmul(o_ps[ob][:], lhsT=merged[:, kc, tt * 128:(tt + 1) * 128],
                                                           rhs=wo[:, kc, half * 512:(half + 1) * 512], start=(kc == 0), stop=(kc == 7))
                                return ins
                            sc.op("pe", mmo, reads=[t_merged, t_w3], writes=[t_ops[ob]])
                            sc.op("dve", lambda: nc.vector.tensor_tensor(out=xs4[xl][:, tt, half * 512:(half + 1) * 512],
                                                                         in0=xs4[xl][:, tt, half * 512:(half + 1) * 512], in1=o_ps[ob][:],
                                                                         op=ALU.add),
                                  reads=[t_ops[ob], t_xs4[xl]], pwrites=[t_xs4[xl]])
                    sc.dma("sp", f"p3xo{xl}", xmid_d[st * 512:(st + 1) * 512, :].rearrange("(tt p) d -> p tt d", p=128), xs4[xl][:],
                           reads=[t_xs4[xl]], pwrites=[t_xmd])

                def O_stats(st):
                    xl, sl = st % 3, st % 2
                    stats(xs4[xl], t_xs4[xl], sB[sl], t_sB[sl])

                def T2(st):
                    xl, sl = st % 3, st % 2
                    norm_T(xs4[xl], t_xs4[xl], sB[sl], t_sB[sl], h2st[sl], t_h2st[sl], 8)
                    sc.dma("sp", f"p3h2{sl}", h2T_d[:, :, st * 512:(st + 1) * 512].rearrange("k p t -> p k t"), h2st[sl][:],
                           reads=[t_h2st[sl]], pwrites=[t_h2d])

                X_load(0)
                if NST > 1:
                    X_load(1)
                X_stats(0)
                X_T(0)
                for st in range(NST):
                    hooks = []
                    if st >= 1:
                        hooks.append((1, (lambda a: (lambda: O_stats(a)))(st - 1)))
                    if st + 1 < NST:
                        hooks.append((4, (lambda a: (lambda: X_stats(a)))(st + 1)))
                    G(st, hooks)
                    if st + 1 < NST:
                        X_T(st + 1)
                    if st >= 1:
                        T2(st - 1)
                    if st + 2 < NST:
                        X_load(st + 2)
                    O(st)
                O_stats(NST - 1)
                T2(NST - 1)
                sc.barrier()
            w3.close()

            with ExitStack() as p4:
                TW = 256
                NT2 = S // TW
                wup = sb("p4_wup", [128, 8, DFF], BF16, p4)
                wdn = sb("p4_wdn", [128, 32, D], BF16, p4)
                t_w4 = Tok()
                t_wup = [Tok() for _ in range(4)]
                for cb in range(4):
                    for kc in range(8):
                        sc.dma("pool", f"w4u{cb}", wup[:, kc, cb * 1024:(cb + 1) * 1024],
                               w_up[l, kc * 128:(kc + 1) * 128, cb * 1024:(cb + 1) * 1024], pwrites=[t_wup[cb]])
                for q4 in range(16):
                    sc.dma("pool", "w4d", wdn[:, q4 * 2:(q4 + 1) * 2, :],
                           w_down[l, q4 * 256:(q4 + 1) * 256, :].rearrange("(kc p) n -> p kc n", p=128), pwrites=[t_w4])
                h2t = [sb(f"p4_h2t{i}", [128, 8, TW], BF16, p4) for i in range(2)]
                x2 = [sb(f"p4_x2{i}", [128, 2, D], F32, p4) for i in range(2)]
                t_in4 = [Tok() for _ in range(2)]
                t_x2 = [Tok() for _ in range(2)]
                aT = sb("p4_aT", [128, 32, TW], BF16, p4)
                t_aT = Tok()
                rl = [sb(f"p4_rl{i}", [128, 512], F32, p4) for i in range(2)]
                t_rl = [Tok() for _ in range(2)]
                up_ps = [ps(f"p4_ups{i}", [128, 2, TW], F32, p4) for i in range(3)]
                t_up = [Tok() for _ in range(3)]
                d_ps = [ps(f"p4_dps{i}", [128, 512], F32, p4) for i in range(4)]
                t_dps = [Tok() for _ in range(4)]
                t_xo = Tok()

                def p4_load(i):
                    sl = i % 2
                    sc.dma("sp", f"p4h{sl}", h2t[sl][:], h2T_d[:, :, i * TW:(i + 1) * TW].rearrange("k p t -> p k t"), writes=[t_in4[sl]])
                    sc.dma("sp", f"p4x{sl}", x2[sl][:], xmid_d[i * TW:(i + 1) * TW, :].rearrange("(tt p) d -> p tt d", p=128),
                           writes=[t_x2[sl]])

                p4_load(0)
                for i in range(NT2):
                    sl = i % 2
                    if i + 1 < NT2:
                        p4_load(i + 1)
                    t_aT.new_gen()
                    for fp in range(16):
                        ub = fp % 3
                        rb = fp % 2

                        def mmup():
                            for sub in range(2):
                                ft = fp * 2 + sub
                                for kc in range(8):
                                    ins = nc.tensor.matmul(up_ps[ub][:, sub, :], lhsT=wup[:, kc, ft * 128:(ft + 1) * 128], rhs=h2t[sl][:, kc, :],
                                                           start=(kc == 0), stop=(kc == 7))
                            return ins
                        sc.op("pe", mmup, reads=[t_in4[sl], t_wup[fp // 4]], writes=[t_up[ub]])
                        sc.op("act", lambda: nc.scalar.activation(out=rl[rb][:], in_=up_ps[ub][:].rearrange("p a t -> p (a t)"), func=AF.Relu),
                              reads=[t_up[ub]], writes=[t_rl[rb]])
                        sc.op("dve", lambda: nc.vector.tensor_tensor(out=aT[:, fp * 2:fp * 2 + 2, :].rearrange("p a t -> p (a t)"), in0=rl[rb][:],
                                                                     in1=rl[rb][:], op=ALU.mult),
                              reads=[t_rl[rb]], pwrites=[t_aT])
                    for tt in range(2):
                        for half in range(2):
                            db = tt * 2 + half

                            def mmd():
                                for ft in range(32):
                                    ins = nc.tensor.matmul(d_ps[db][:], lhsT=aT[:, ft, tt * 128:(tt + 1) * 128],
                                                           rhs=wdn[:, ft, half * 512:(half + 1) * 512], start=(ft == 0), stop=(ft == 31))
                                return ins
                            sc.op("pe", mmd, reads=[t_aT, t_w4], writes=[t_dps[db]])
                            sc.op("dve", lambda: nc.vector.tensor_tensor(out=x2[sl][:, tt, half * 512:(half + 1) * 512],
                                                                         in0=x2[sl][:, tt, half * 512:(half + 1) * 512], in1=d_ps[db][:], op=ALU.add),
                                  reads=[t_dps[db], t_x2[sl]], pwrites=[t_x2[sl]])
                    sc.dma("sp", f"p4o{sl}", x_dst[i * TW:(i + 1) * TW, :].rearrange("(tt p) d -> p tt d", p=128), x2[sl][:],
                           reads=[t_x2[sl]], pwrites=[t_xo])
                sc.barrier()
        sc.final_wait("sp")
    return nc


def _constants(S):
    bf = ml_dtypes.bfloat16
    pos = np.arange(S)
    ql = (pos % 128).astype(np.float32)
    qb = (pos // 128).astype(np.float32)
    ones = np.ones(S, np.float32)
    qconst = np.stack([-ql, ones, -128.0 * qb, ones]).astype(bf)
    slopes = np.array([2.0 ** (-8.0 * (i + 1) / NH) for i in range(NH)], np.float32)
    kconst = np.stack([np.stack([sl * ones, sl * ql, sl * ones, sl * 128.0 * qb]) for sl in slopes]).astype(bf)
    kl = np.arange(128)[:, None]
    qq = np.arange(128)[None, :]
    diag = np.zeros((128, NH, 128), np.float32)
    mask = (qq < 64) & (kl >= 64)
    for h in range(NH):
        d = -2.0 * slopes[h] * np.maximum(kl - qq, 0).astype(np.float32)
        diag[:, h, :] = np.where(mask, NEG, d)
    rc = np.zeros((128, 4, 16), np.float32)
    for g, w in enumerate(POOL_WINDOWS):
        rc[:, g, :] = 1.0 / np.minimum(np.arange(16) + 1, w).astype(np.float32)
    ident = np.eye(128, dtype=np.float32).astype(bf)
    return dict(ident=ident, qconst=qconst, kconst=kconst, diagT=diag.astype(bf), poolrc=rc)


def _shared_inputs(S, L, g_mix, w_in, w_pool_grp, pool_scale, g_q, g_k, lambda_qk, g_sub, w_branch_pool, w_branch_attn,
                   w_gate, b_gate, w_out, g_ffn, w_up, w_down):
    f = lambda a: np.ascontiguousarray(np.asarray(a, dtype=np.float32))
    cols = lambda v, n: f(v).reshape(L, n, 128).transpose(0, 2, 1)
    d = dict(
        w_in=f(w_in), w_pool_grp=f(w_pool_grp).reshape(L, 512, 128), w_branch_pool=f(w_branch_pool), w_branch_attn=f(w_branch_attn),
        w_gate=f(w_gate), w_out=f(w_out), w_up=f(w_up), w_down=f(w_down),
        gcols=np.ascontiguousarray(np.concatenate([cols(g_mix, 8), cols(g_ffn, 8)], axis=2)),
        bgate=np.ascontiguousarray(cols(b_gate, 16)),
        pscale=np.ascontiguousarray(cols(pool_scale, 4)),
        gqk=np.ascontiguousarray(np.stack([np.tile(f(g_q), (1, 2)), np.tile(f(g_k), (1, 2))], axis=2)),
        gsub=np.ascontiguousarray(f(g_sub).reshape(L, 128, 1)),
        lamb=np.ascontiguousarray(np.broadcast_to(f(lambda_qk).reshape(L, 1, 256), (L, 128, 256))),
    )
    d.update(_constants(S))
    return d


def kernel(x, g_mix, w_in, w_pool_grp, pool_scale, g_q, g_k, lambda_qk, g_sub, w_branch_pool, w_branch_attn,
           w_gate, b_gate, w_out, g_ffn, w_up, w_down):
    x = np.asarray(x, dtype=np.float32)
    B, S, _ = x.shape
    L = np.asarray(g_mix).shape[0]
    shared = _shared_inputs(S, L, g_mix, w_in, w_pool_grp, pool_scale, g_q, g_k, lambda_qk, g_sub, w_branch_pool,
                            w_branch_attn, w_gate, b_gate, w_out, g_ffn, w_up, w_down)
    nc = build_program(S=S, L=L)
    in_maps = [dict(shared, x=np.ascontiguousarray(x[b])) for b in range(B)]
    res = run_bass_kernel_spmd(nc, in_maps, core_ids=list(range(B)))
    return np.stack([np.asarray(r["y"], dtype=np.float32) for r in res.results], axis=0)
```
